# Optimizing a Trainium2 kernel written in Bass

```python
import math
import jax, jax.numpy as jnp
from jax import lax
import numpy as np

D_MODEL = 1024
BATCH = 16
SEQ = 4096
DEPTH = 1
DEC_BATCH = 32
DEC_SEQ = 16
PAST_LEN = 2048

CHUNK = 64
HEAD_DIM = 64
N_HEADS = 8
N_KV_HEADS = 2
GROUP = N_HEADS // N_KV_HEADS
ATT_WIDTH = N_HEADS * HEAD_DIM
KV_WIDTH = N_KV_HEADS * HEAD_DIM
CONV_WIDTH = D_MODEL - ATT_WIDTH
CONV_K = 31
WINDOW = 128
N_PREV_CHUNKS = WINDOW // CHUNK
BAND = (N_PREV_CHUNKS + 1) * CHUNK
IN_WIDTH = ATT_WIDTH + 2 * KV_WIDTH + 2 * CONV_WIDTH
D_FF = 2816
NUM_BUCKETS = 32
MAX_DISTANCE = 128
EPS = 1e-6

kernel_name = "hymba_conformer_swa_stream_step"


def rmsnorm(x, g):
    xf = x.astype(jnp.float32)
    xf = xf * lax.rsqrt(jnp.mean(xf * xf, axis=-1, keepdims=True) + EPS)
    return (xf * g.astype(jnp.float32)).astype(x.dtype)


def layernorm(x, g, b):
    xf = x.astype(jnp.float32)
    mu = jnp.mean(xf, axis=-1, keepdims=True)
    xc = xf - mu
    var = jnp.mean(xc * xc, axis=-1, keepdims=True)
    return (xc * lax.rsqrt(var + EPS) * g.astype(jnp.float32) + b.astype(jnp.float32)).astype(x.dtype)


def swiglu_ffn(x, norm_g, w_gu, w_down):
    h = rmsnorm(x, norm_g) @ w_gu
    gate, up = jnp.split(h, 2, axis=-1)
    return (jax.nn.silu(gate) * up) @ w_down


def rel_bucket(rel):
    nb = NUM_BUCKETS // 2
    max_exact = nb // 2
    ret = jnp.where(rel > 0, nb, 0)
    n = jnp.abs(rel)
    nf = jnp.maximum(n, 1).astype(jnp.float32)
    large = max_exact + (jnp.log(nf / max_exact) / math.log(MAX_DISTANCE / max_exact)
                         * (nb - max_exact)).astype(jnp.int32)
    large = jnp.minimum(large, nb - 1)
    return ret + jnp.where(n < max_exact, n, large)


def rel_bias(table, qpos, kpos):
    b = rel_bucket(kpos[None, :] - qpos[:, None])
    return jnp.transpose(table[b], (2, 0, 1)).astype(jnp.float32)


def project(h, w_in, q_gain, k_gain):
    B, L, _ = h.shape
    z = h @ w_in
    q, k, v, a, g = jnp.split(
        z, [ATT_WIDTH, ATT_WIDTH + KV_WIDTH, ATT_WIDTH + 2 * KV_WIDTH,
            ATT_WIDTH + 2 * KV_WIDTH + CONV_WIDTH], axis=-1)
    q = rmsnorm(q.reshape(B, L, N_HEADS, HEAD_DIM), q_gain) * (HEAD_DIM ** -0.5)
    k = rmsnorm(k.reshape(B, L, N_KV_HEADS, HEAD_DIM), k_gain)
    v = v.reshape(B, L, N_KV_HEADS, HEAD_DIM)
    u = a * jax.nn.sigmoid(g)
    return q, k, v, u


def sink_attend(qb, kb, vb, bias, mask, sinks):
    LQ, LK = qb.shape[2], kb.shape[2]
    s = jnp.einsum('bnqkgd,bnskd->bnkgqs', qb, kb, preferred_element_type=jnp.float32)
    s = s + bias.reshape(N_KV_HEADS, GROUP, LQ, LK)
    s = jnp.where(mask[None, :, None, None, None, :], s, -jnp.inf)
    sink = jnp.broadcast_to(sinks.astype(jnp.float32).reshape(N_KV_HEADS, GROUP, 1, 1),
                            s.shape[:-1] + (1,))
    p = jax.nn.softmax(jnp.concatenate([s, sink], axis=-1), axis=-1)[..., :-1]
    return jnp.einsum('bnkgqs,bnskd->bnqkgd', p.astype(vb.dtype), vb)


def conv_tail(u_ext, conv_w, conv_b, ln_g, ln_b):
    y = lax.conv_general_dilated(u_ext, conv_w[:, None, :].astype(u_ext.dtype), (1,), 'VALID',
                                 dimension_numbers=('NWC', 'WIO', 'NWC'),
                                 feature_group_count=u_ext.shape[-1])
    y = layernorm(y + conv_b, ln_g, ln_b)
    return jax.nn.silu(y)


def setup_inputs(seed: int = 0) -> dict:
    key = jax.random.key(seed)
    ks = jax.random.split(key, 24)
    f32 = jnp.float32
    R = min(WINDOW, PAST_LEN)
    nrm = lambda k, shape, s: jax.random.normal(k, shape, f32) * s
    gain = lambda k, shape: 1.0 + 0.05 * jax.random.normal(k, shape, f32)
    return {
        "x_prompt": nrm(ks[0], (BATCH, SEQ, D_MODEL), 1.0),
        "x_sample": nrm(ks[1], (DEC_BATCH, DEC_SEQ, D_MODEL), 1.0),
        "cache_k": nrm(ks[2], (DEPTH, DEC_BATCH, R, N_KV_HEADS, HEAD_DIM), 1.0),
        "cache_v": nrm(ks[3], (DEPTH, DEC_BATCH, R, N_KV_HEADS, HEAD_DIM), 1.0),
        "state_conv": nrm(ks[4], (DEPTH, DEC_BATCH, CONV_K - 1, CONV_WIDTH), 0.5),
        "rel_bias_table": nrm(ks[5], (NUM_BUCKETS, N_HEADS), 0.5),
        "ffn1_norm": gain(ks[6], (DEPTH, D_MODEL)),
        "ffn1_w_gu": nrm(ks[7], (DEPTH, D_MODEL, 2 * D_FF), D_MODEL ** -0.5),
        "ffn1_w_down": nrm(ks[8], (DEPTH, D_FF, D_MODEL), D_FF ** -0.5),
        "mix_norm": gain(ks[9], (DEPTH, D_MODEL)),
        "w_in": nrm(ks[10], (DEPTH, D_MODEL, IN_WIDTH), D_MODEL ** -0.5),
        "q_norm": gain(ks[11], (DEPTH, HEAD_DIM)),
        "k_norm": gain(ks[12], (DEPTH, HEAD_DIM)),
        "sinks": nrm(ks[13], (DEPTH, N_HEADS), 0.5),
        "conv_w": nrm(ks[14], (DEPTH, CONV_K, CONV_WIDTH), CONV_K ** -0.5),
        "conv_b": nrm(ks[15], (DEPTH, CONV_WIDTH), 0.02),
        "conv_ln_g": gain(ks[16], (DEPTH, CONV_WIDTH)),
        "conv_ln_b": nrm(ks[17], (DEPTH, CONV_WIDTH), 0.02),
        "w_out": nrm(ks[18], (DEPTH, D_MODEL, D_MODEL), D_MODEL ** -0.5),
        "ffn2_norm": gain(ks[19], (DEPTH, D_MODEL)),
        "ffn2_w_gu": nrm(ks[20], (DEPTH, D_MODEL, 2 * D_FF), D_MODEL ** -0.5),
        "ffn2_w_down": nrm(ks[21], (DEPTH, D_FF, D_MODEL), D_FF ** -0.5),
        "final_norm": gain(ks[22], (DEPTH, D_MODEL)),
    }


def reference(x_prompt, x_sample, cache_k, cache_v, state_conv, rel_bias_table,
              ffn1_norm, ffn1_w_gu, ffn1_w_down, mix_norm, w_in, q_norm, k_norm, sinks,
              conv_w, conv_b, conv_ln_g, conv_ln_b, w_out, ffn2_norm, ffn2_w_gu,
              ffn2_w_down, final_norm):
    B, S, _ = x_prompt.shape
    DB, DS, _ = x_sample.shape
    R = cache_k.shape[2]
    RP = min(WINDOW, S)
    nc = S // CHUNK

    bias_p = rel_bias(rel_bias_table, jnp.arange(CHUNK), jnp.arange(BAND) - WINDOW)
    kabs = jnp.arange(nc)[:, None] * CHUNK - WINDOW + jnp.arange(BAND)[None, :]
    mask_p = kabs >= 0
    qpos_s = PAST_LEN + jnp.arange(DS)
    kpos_s = jnp.concatenate([PAST_LEN - R + jnp.arange(R), qpos_s])
    bias_s = rel_bias(rel_bias_table, qpos_s, kpos_s)
    mask_s = jnp.ones((1, R + DS), dtype=bool)

    xp, xs = x_prompt, x_sample
    nkp, nvp, ncp, nks, nvs, ncs = [], [], [], [], [], []
    for l in range(DEPTH):
        xp = xp + 0.5 * swiglu_ffn(xp, ffn1_norm[l], ffn1_w_gu[l], ffn1_w_down[l])
        xs = xs + 0.5 * swiglu_ffn(xs, ffn1_norm[l], ffn1_w_gu[l], ffn1_w_down[l])

        q, k, v, u = project(rmsnorm(xp, mix_norm[l]), w_in[l], q_norm[l], k_norm[l])
        pad = ((0, 0), (WINDOW, 0), (0, 0), (0, 0))
        kp = jnp.pad(k, pad).reshape(B, nc + N_PREV_CHUNKS, CHUNK, N_KV_HEADS, HEAD_DIM)
        vp = jnp.pad(v, pad).reshape(B, nc + N_PREV_CHUNKS, CHUNK, N_KV_HEADS, HEAD_DIM)
        kb = jnp.concatenate([kp[:, i:i + nc] for i in range(N_PREV_CHUNKS + 1)], axis=2)
        vb = jnp.concatenate([vp[:, i:i + nc] for i in range(N_PREV_CHUNKS + 1)], axis=2)
        qb = q.reshape(B, nc, CHUNK, N_KV_HEADS, GROUP, HEAD_DIM)
        att = sink_attend(qb, kb, vb, bias_p, mask_p, sinks[l]).reshape(B, S, ATT_WIDTH)
        u_ext = jnp.pad(u, ((0, 0), (CONV_K - 1, 0), (0, 0)))
        cy = conv_tail(u_ext, conv_w[l], conv_b[l], conv_ln_g[l], conv_ln_b[l])
        xp = xp + jnp.concatenate([att, cy], axis=-1) @ w_out[l]
        nkp.append(k[:, S - RP:])
        nvp.append(v[:, S - RP:])
        ncp.append(u[:, S - (CONV_K - 1):])

        q, k, v, u = project(rmsnorm(xs, mix_norm[l]), w_in[l], q_norm[l], k_norm[l])
        k_all = jnp.concatenate([cache_k[l].astype(k.dtype), k], axis=1)
        v_all = jnp.concatenate([cache_v[l].astype(v.dtype), v], axis=1)
        qb = q.reshape(DB, 1, DS, N_KV_HEADS, GROUP, HEAD_DIM)
        att = sink_attend(qb, k_all[:, None], v_all[:, None], bias_s, mask_s,
                          sinks[l]).reshape(DB, DS, ATT_WIDTH)
        u_ext = jnp.concatenate([state_conv[l].astype(u.dtype), u], axis=1)
        cy = conv_tail(u_ext, conv_w[l], conv_b[l], conv_ln_g[l], conv_ln_b[l])
        xs = xs + jnp.concatenate([att, cy], axis=-1) @ w_out[l]
        nks.append(k_all[:, DS:])
        nvs.append(v_all[:, DS:])
        ncs.append(u_ext[:, DS:])

        xp = xp + 0.5 * swiglu_ffn(xp, ffn2_norm[l], ffn2_w_gu[l], ffn2_w_down[l])
        xs = xs + 0.5 * swiglu_ffn(xs, ffn2_norm[l], ffn2_w_gu[l], ffn2_w_down[l])
        xp = rmsnorm(xp, final_norm[l])
        xs = rmsnorm(xs, final_norm[l])

    return (xp, xs, jnp.stack(nkp), jnp.stack(nvp), jnp.stack(ncp),
            jnp.stack(nks), jnp.stack(nvs), jnp.stack(ncs))
```

```python
import numpy as np
import concourse.bass as bass
import concourse.mybir as mybir
from concourse.bass_utils import run_bass_kernel_spmd

F32 = mybir.dt.float32
BF16 = mybir.dt.bfloat16
AF = mybir.ActivationFunctionType
ALU = mybir.AluOpType

NCORES = 8
D = 1024
KD = 8
DFF = 2816
NM = 22
SEQ = 4096
NSEQ = 2
GT = 512
NGRP = SEQ // GT
SB = 4
DS = 16
R = 128
CK = 31
EPS = 1e-6
NXS = 8
NDVE = 8
STRICT = False


class Buf:
    __slots__ = ("name", "w", "r")

    def __init__(self, name):
        self.name = name
        self.w = None
        self.r = {}


class Ring:
    def __init__(self, items):
        self.items = items
        self.i = 0

    def next(self):
        it = self.items[self.i % len(self.items)]
        self.i += 1
        return it


class Sched:
    def __init__(self, nc):
        self.nc = nc
        self.eng = {'pe': nc.tensor, 'act': nc.scalar, 'dve': nc.vector, 'pool': nc.gpsimd, 'sp': nc.sync}
        self.sem = {e: nc.alloc_semaphore("sem_" + e) for e in self.eng}
        self.cnt = {e: 0 for e in self.eng}
        self.known = {e: {} for e in self.eng}
        self.dsem = {}
        self.dcnt = {}

    def _wait(self, e, toks):
        need = {}
        for (k, s, v) in toks:
            if k not in need or need[k][1] < v:
                need[k] = (s, v)
        for k, (s, v) in need.items():
            if self.known[e].get(k, 0) >= v:
                continue
            self.eng[e].wait_ge(s, v)
            self.known[e][k] = v

    def _deps(self, e, reads, writes, skip_waw=False):
        toks = []
        own = "sem_" + e
        for b in reads:
            if b.w is not None:
                toks.append(b.w)
        for b in writes:
            if b.w is not None and (STRICT or b.w[0] != own) and not skip_waw:
                toks.append(b.w)
            for k, (s, v) in b.r.items():
                if STRICT or k != own:
                    toks.append((k, s, v))
        return toks

    def _mark(self, tok, reads, writes):
        for b in reads:
            b.r[tok[0]] = (tok[1], tok[2])
        for b in writes:
            b.w = tok
            b.r = {}

    def op(self, e, fn, reads=(), writes=()):
        self._wait(e, self._deps(e, reads, writes))
        ins = fn(self.eng[e])
        self.cnt[e] += 1
        ins.then_inc(self.sem[e], 1)
        tok = ("sem_" + e, self.sem[e], self.cnt[e])
        self._mark(tok, reads, writes)
        return tok

    def dma(self, q, semname, fns, reads=(), writes=(), skip_waw=False):
        if semname not in self.dsem:
            self.dsem[semname] = self.nc.alloc_semaphore("ds_" + semname)
            self.dcnt[semname] = 0
        s = self.dsem[semname]
        self._wait(q, self._deps(q, reads, writes, skip_waw))
        for fn in fns:
            ins = fn(self.eng[q])
            self.dcnt[semname] += 16
            ins.then_inc(s, 16)
        tok = ("ds_" + semname, s, self.dcnt[semname])
        self._mark(tok, reads, writes)
        return tok

    def barrier(self):
        for e in self.eng:
            for name, s in self.dsem.items():
                if self.dcnt[name] > self.known[e].get("ds_" + name, 0):
                    self.eng[e].wait_ge(s, self.dcnt[name])
                    self.known[e]["ds_" + name] = self.dcnt[name]
            for en in self.eng:
                if en != e and self.cnt[en] > self.known[e].get("sem_" + en, 0):
                    self.eng[e].wait_ge(self.sem[en], self.cnt[en])
                    self.known[e]["sem_" + en] = self.cnt[en]

    def finish(self, e='sp'):
        for name, s in self.dsem.items():
            if self.dcnt[name] > 0:
                self.eng[e].wait_ge(s, self.dcnt[name])
        for en in self.eng:
            if en != e and self.cnt[en] > 0:
                self.eng[e].wait_ge(self.sem[en], self.cnt[en])


def build_nc():
    nc = bass.Bass("TRN2", target_bir_lowering=False)
    S = Sched(nc)

    def din(name, shape, dt=F32):
        return nc.dram_tensor(name, list(shape), dt, kind="ExternalInput").ap()

    def dout(name, shape):
        return nc.dram_tensor(name, list(shape), F32, kind="ExternalOutput").ap()

    xp = din("xp", [NSEQ * SEQ, D])
    xsm = din("xsm", [SB * DS, D])
    ck = din("ck", [SB, R, 128])
    cv = din("cv", [SB, R, 128])
    sc = din("sc", [SB, CK - 1, 512])
    rbt = din("rbt", [32, 8])
    gains_d = [din("ffn1_norm", [1, D]), din("mix_norm", [1, D]), din("ffn2_norm", [1, D])]
    wgu_d = [din("ffn1_w_gu", [D, 2 * DFF]), din("ffn2_w_gu", [D, 2 * DFF])]
    wd_d = [din("ffn1_w_down", [DFF, D]), din("ffn2_w_down", [DFF, D])]
    win_d = din("w_in", [D, 1792])
    qn_d = din("q_norm", [1, 64])
    kn_d = din("k_norm", [1, 64])
    sinks_d = din("sinks", [1, 8])
    convw_d = din("conv_w", [CK, 512])
    convb_d = din("conv_b", [1, 512])
    lng_d = din("conv_ln_g", [1, 512])
    lnb_d = din("conv_ln_b", [1, 512])
    wout_d = din("w_out", [D, D])
    fin_d = din("final_norm", [1, D])
    ident_d = din("ident", [128, 128])
    jmat_d = din("jmat", [128, 128])
    ohr_d = din("ohr", [32, 512])

    yp = dout("yp", [NSEQ * SEQ, D])
    ys = dout("ys", [SB * DS, D])
    nkp = dout("nkp", [NSEQ, R, 128])
    nvp = dout("nvp", [NSEQ, R, 128])
    ncp = dout("ncp", [NSEQ, CK - 1, 512])
    nks = dout("nks", [SB, R, 128])
    nvs = dout("nvs", [SB, R, 128])
    ncs = dout("ncs", [SB, CK - 1, 512])

    s_gu = [nc.dram_tensor("s_gu%d" % f, [NM, 128, KD * 256], BF16).ap() for f in range(2)]
    s_d = [nc.dram_tensor("s_d%d" % f, [NM, 128, D], BF16).ap() for f in range(2)]
    s_in = nc.dram_tensor("s_in", [7, 128, KD * 256], BF16).ap()
    s_o = nc.dram_tensor("s_o", [KD, 128, D], BF16).ap()
    vr_d = nc.dram_tensor("vr_d", [8, 512], F32).ap()

    def sb(name, shape, dt=F32):
        return nc.alloc_sbuf_tensor("sb_" + name, list(shape), dt).ap()

    def ps(name, shape, dt=F32):
        return nc.alloc_psum_tensor("ps_" + name, list(shape), dt).ap()

    xs = [sb("xs%d" % i, [128, D]) for i in range(NXS)]
    xs_b = [Buf("xs%d" % i) for i in range(NXS)]
    gfin = sb("gfin", [128, D]); gfin_b = Buf("gfin")
    xq = [(sb("xnT%d" % i, [128, KD, GT], BF16), Buf("xnT%d" % i)) for i in range(2)]
    junk = sb("junk", [128, D], BF16); junk_b = Buf("junk")
    hT = sb("hT", [128, NM, GT], BF16); hT_bA = Buf("hTa"); hT_bB = Buf("hTb")
    MSPLIT = 18
    Dbuf = sb("Dbuf", [128, NM, D], BF16)
    D_b = [Buf("D%d" % m) for m in range(NM)]
    ring = Ring([(sb("ring%d" % i, [128, KD, 256], BF16), Buf("ring%d" % i), "ring%d" % i) for i in range(4)])
    xn_ring = Ring([(sb("xn%d" % i, [128, D], BF16), Buf("xn%d" % i)) for i in range(2)])
    tmpf = Ring([(sb("tf%d" % i, [128, 512]), Buf("tf%d" % i)) for i in range(6)])
    tmpb = Ring([(sb("tb%d" % i, [128, 512], BF16), Buf("tb%d" % i)) for i in range(4)])
    smr = Ring([(sb("sm%d" % i, [128, 16]), Buf("sm%d" % i)) for i in range(12)])
    qT = sb("qT", [128, 4, GT], BF16); qT_b = Buf("qT")
    kTe = sb("kTe", [128, 128 + GT], BF16); kTe_b = Buf("kTe")
    Vp = sb("Vp", [128, 5, 2, 128], BF16); Vp_b = Buf("Vp")
    uext = sb("uext", [128, 4, CK - 1 + GT], BF16); uext_b = Buf("uext")
    utail = sb("utail", [128, 4, 32]); utail_b = Buf("utail")
    dgr = Ring([(sb("dg%d" % i, [128, 128], BF16), Buf("dg%d" % i)) for i in range(12)])
    onec = sb("onec", [128, 1])
    acc = sb("acc", [128, 4, GT]); acc_b = Buf("acc"); acc_cb = [Buf("acc%d" % c) for c in range(4)]
    catT = sb("catT", [128, KD, GT], BF16); catA_b = Buf("catA"); catC_b = Buf("catC")
    ebias = [[sb("eb%d%d" % (g, kt), [128, 4, 128], BF16) for kt in range(2)] for g in range(2)]
    ebias_b = Buf("ebias")
    sinkexp = sb("sinkexp", [128, 4, 128]); cst_b = Buf("consts")
    identf = sb("identf", [128, 128]); identb = sb("identb", [128, 128], BF16)
    hTf = hT.rearrange("p m n -> p (m n)").bitcast(F32)
    jm = hTf[:, 0:128]; onesf = sb("onesf", [128, 128])
    blockones = sb("blockones", [128, 128], BF16)
    onespad = sb("onespad", [128, 2, 128], BF16)
    nhalf = sb("nhalf", [128, 8]); zeros = hTf[:, 1936:2064]
    prm = hTf[0:34, 1152:1664]; prm2 = hTf[0:24, 1664:1792]; prm3 = hTf[0:2, 1792:1920]
    prmT = sb("prmT", [128, 4, 34]); gcols = sb("gcols", [128, 24]); qkc = sb("qkc", [128, 2])
    sk = sb("sk", [128, 4]); ske = sb("ske", [128, 4]); epsc = sb("epsc", [128, 1])
    tbt = hTf[0:32, 1920:1928]; etab = hTf[0:32, 1928:1936]; ohr = hTf[0:32, 128:640]; vrs = hTf[0:8, 640:1152]
    kTc = sb("kTc", [128, SB, 128], BF16); Vpc = sb("Vpc", [128, SB, 2, 128], BF16)
    Vsn = sb("Vsn", [16, SB, 2, 128], BF16); uexs = sb("uexs", [128, 4, SB, CK - 1 + DS])
    smp_b = Buf("smp")

    pT_ring = Ring([(ps("pT%d" % i, [128, D], BF16), Buf("pT%d" % i)) for i in range(2)])
    P_items = [(ps("P%d" % i, [128, 512]), Buf("P%d" % i)) for i in range(6)]
    P_ring = Ring(P_items)
    PO = (P_items[0], P_items[1])
    PG_ring = Ring(P_items[2:5])
    PS_ring = Ring([(pT_ring.items[0][0].bitcast(F32), pT_ring.items[0][1]), (pT_ring.items[1][0].bitcast(F32), pT_ring.items[1][1]), P_items[5]])

    act = lambda fn, r=(), w=(): S.op('act', fn, r, w)
    dve = lambda fn, r=(), w=(): S.op('dve', fn, r, w)
    pool = lambda fn, r=(), w=(): S.op('pool', fn, r, w)
    pe = lambda fn, r=(), w=(): S.op('pe', fn, r, w)

    def mmgroup(out, pairs, first=True, last=True):
        def fn(e):
            ins = None
            n = len(pairs)
            for i, (l, r_) in enumerate(pairs):
                ins = e.matmul(out, lhsT=l, rhs=r_, start=(first and i == 0), stop=(last and i == n - 1))
            return ins
        return fn

    S.dma('sp', 'c0', [lambda e: e.dma_start(out=identf, in_=ident_d),
                       lambda e: e.dma_start(out=jm, in_=jmat_d),
                       lambda e: e.dma_start(out=ohr, in_=ohr_d),
                       lambda e: e.dma_start(out=tbt, in_=rbt),
                       lambda e: e.dma_start(out=gfin, in_=fin_d.to_broadcast([128, D])),
                       lambda e: e.dma_start(out=prm[0:CK, :], in_=convw_d),
                       lambda e: e.dma_start(out=prm[31:32, :], in_=convb_d),
                       lambda e: e.dma_start(out=prm[32:33, :], in_=lng_d),
                       lambda e: e.dma_start(out=prm[33:34, :], in_=lnb_d),
                       lambda e: e.dma_start(out=prm2[0:8, :], in_=gains_d[0].rearrange("o (k p) -> (o k) p", p=128)),
                       lambda e: e.dma_start(out=prm2[8:16, :], in_=gains_d[1].rearrange("o (k p) -> (o k) p", p=128)),
                       lambda e: e.dma_start(out=prm2[16:24, :], in_=gains_d[2].rearrange("o (k p) -> (o k) p", p=128)),
                       lambda e: e.dma_start(out=prm3[0:1, 0:64], in_=qn_d),
                       lambda e: e.dma_start(out=prm3[0:1, 64:128], in_=qn_d),
                       lambda e: e.dma_start(out=prm3[1:2, 0:64], in_=kn_d),
                       lambda e: e.dma_start(out=prm3[1:2, 64:128], in_=kn_d),
                       lambda e: e.dma_start(out=sk[0:64, :], in_=sinks_d[:, 0:4].to_broadcast([64, 4])),
                       lambda e: e.dma_start(out=sk[64:128, :], in_=sinks_d[:, 4:8].to_broadcast([64, 4])),
                       ], writes=[cst_b, gfin_b])
    dve(lambda e: e.tensor_copy(out=identb, in_=identf), [cst_b], [cst_b])
    dve(lambda e: e.memset(onesf, 1.0), [], [cst_b])
    dve(lambda e: e.memset(zeros, 0.0), [], [cst_b])
    dve(lambda e: e.memset(epsc, EPS), [], [cst_b])
    dve(lambda e: e.memset(onec, 1.0), [], [cst_b])
    dve(lambda e: e.memset(nhalf, -0.5), [], [cst_b])
    dve(lambda e: e.memset(blockones, 0.0), [], [cst_b])
    dve(lambda e: e.memset(onespad, 0.0), [], [cst_b])
    dve(lambda e: e.memset(Vp, 0.0), [], [Vp_b])
    dve(lambda e: e.memset(Vpc, 0.0), [], [smp_b])
    dve(lambda e: e.memset(Vsn, 0.0), [], [smp_b])
    dve(lambda e: e.memset(blockones[0:64, 0:64], 1.0), [cst_b], [cst_b])
    dve(lambda e: e.memset(blockones[64:128, 64:128], 1.0), [cst_b], [cst_b])
    dve(lambda e: e.memset(onespad[:, 0, 0:64], 1.0), [cst_b], [cst_b])
    dve(lambda e: e.memset(onespad[:, 1, 64:128], 1.0), [cst_b], [cst_b])
    pp, pp_b = P_ring.next()
    for c in range(4):
        pe(lambda e: e.transpose(out=pp[:, c * 34:(c + 1) * 34], in_=prm[0:34, c * 128:(c + 1) * 128],
                                 identity=identf[0:34, 0:34]), [cst_b], [pp_b])
    dve(lambda e: e.tensor_copy(out=prmT, in_=pp[:, 0:136].rearrange("p (c j) -> p c j", c=4)), [pp_b], [cst_b])
    pp, pp_b = P_ring.next()
    pe(lambda e: e.transpose(out=pp[:, 0:24], in_=prm2[0:24, :], identity=identf[0:24, 0:24]), [cst_b], [pp_b])
    pe(lambda e: e.transpose(out=pp[:, 32:34], in_=prm3[0:2, :], identity=identf[0:2, 0:2]), [cst_b], [pp_b])
    dve(lambda e: e.tensor_copy(out=gcols, in_=pp[:, 0:24]), [pp_b], [cst_b])
    dve(lambda e: e.tensor_scalar(out=qkc[:, 0:1], in0=pp[:, 32:33], scalar1=0.125, scalar2=None, op0=ALU.mult), [pp_b], [cst_b])
    dve(lambda e: e.tensor_copy(out=qkc[:, 1:2], in_=pp[:, 33:34]), [pp_b], [cst_b])
    act(lambda e: e.activation(out=ske, in_=sk, func=AF.Exp), [cst_b], [cst_b])
    for a in range(4):
        dve(lambda e: e.tensor_scalar(out=sinkexp[:, a, :], in0=zeros, scalar1=ske[:, a:a + 1], scalar2=None, op0=ALU.add),
            [cst_b], [cst_b])
    act(lambda e: e.copy(out=etab, in_=tbt), [cst_b], [cst_b])
    pp, pp_b = P_ring.next()
    pe(lambda e: e.matmul(pp[0:8, :], lhsT=etab, rhs=ohr, start=True, stop=True), [cst_b], [pp_b])
    dve(lambda e: e.tensor_copy(out=vrs, in_=pp[0:8, :]), [pp_b], [cst_b])
    vr_b = Buf("vr")
    S.dma('sp', 'vr', [lambda e: e.dma_start(out=vr_d, in_=vrs)], reads=[cst_b], writes=[vr_b])
    for g in range(2):
        for kt in range(2):
            hk, hk_b = tmpf.next()
            base = 256 if kt == 0 else 128
            src = bass.AP(vr_d.tensor, 4 * g * 512 + base, [[1, 128], [512, 4], [1, 128]])
            S.dma('sp', 'hk%d%d' % (g, kt), [lambda e: e.dma_start(out=hk.rearrange("p (a q) -> p a q", a=4), in_=src)],
                  reads=[vr_b], writes=[hk_b])
            pp, pp_b = P_ring.next()
            pe(lambda e: e.matmul(pp, lhsT=jm, rhs=hk, start=True, stop=True), [cst_b, hk_b], [pp_b])
            eb = ebias[g][kt]
            dve(lambda e: e.tensor_copy(out=eb, in_=pp.rearrange("p (a q) -> p a q", a=4)), [pp_b], [ebias_b])
            if kt == 0:
                dve(lambda e: e.memset(eb[0:64, :, 64:128], -30000.0), [ebias_b], [ebias_b])
            else:
                dve(lambda e: e.memset(eb[64:128, :, 0:64], -30000.0), [ebias_b], [ebias_b])

    Dflat = Dbuf.rearrange("p m n -> p (m n)")
    stg = [(Dflat[:, i * 5632:(i + 1) * 5632].bitcast(F32), Buf("stg%d" % i)) for i in range(3)]
    outb = [(Dflat[:, 16896 + i * 2816:16896 + (i + 1) * 2816], Buf("outb%d" % i)) for i in range(2)]
    job = [0]

    def prepass(srcs, conv, dsts, hook=None):
        i = job[0] % 2
        i3 = job[0] % 3
        job[0] += 1
        st, st_b = stg[i3]
        ob, ob_b = outb[i]
        S.dma('sp', 'ppin%d' % i3, [(lambda e, f=f: f(e, st)) for f in srcs], writes=[st_b])
        conv(st, st_b, ob, ob_b, i)
        S.dma('pool', 'ppout%d' % i, [(lambda e, f=f: f(e, ob)) for f in dsts], reads=[ob_b])
        if hook is not None:
            hook(ob, ob_b)

    def cvt(eng, o, i_, col, r, w):
        if eng == 'act':
            if col is None:
                act(lambda e: e.copy(out=o, in_=i_), r, w)
            else:
                act(lambda e: e.activation(out=o, in_=i_, func=AF.Copy, scale=col), r, w)
        else:
            if col is None:
                dve(lambda e: e.tensor_copy(out=o, in_=i_), r, w)
            else:
                dve(lambda e: e.tensor_scalar(out=o, in0=i_, scalar1=col, scalar2=None, op0=ALU.mult), r, w)

    def pp_gu(f, hook):
        gi = 0 if f == 0 else 2
        for k in range(KD):
            col = gcols[:, gi * 8 + k:gi * 8 + k + 1]
            rows = slice(k * 128, (k + 1) * 128)
            for mh in range(2):
                c0 = mh * 1408

                def conv(st, st_b, ob, ob_b, i, col=col):
                    o3 = ob.rearrange("p (m c) -> p m c", c=256)
                    cvt('act', o3[:, :, 0:128], st[:, 0:1408].rearrange("p (m c) -> p m c", c=128), col, [st_b, cst_b], [ob_b])
                    cvt('dve', o3[:, :, 128:256], st[:, 1408:2816].rearrange("p (m c) -> p m c", c=128), col, [st_b, cst_b], [ob_b])
                prepass([lambda e, st, c0=c0, rows=rows, f=f: e.dma_start(out=st[:, 0:1408], in_=wgu_d[f][rows, c0:c0 + 1408]),
                         lambda e, st, c0=c0, rows=rows, f=f: e.dma_start(out=st[:, 1408:2816], in_=wgu_d[f][rows, DFF + c0:DFF + c0 + 1408])],
                        conv,
                        [lambda e, ob, mh=mh, k=k, f=f: e.dma_start(
                            out=s_gu[f][mh * 11:(mh + 1) * 11, :, k * 256:(k + 1) * 256].rearrange("m p c -> p m c"),
                            in_=ob.rearrange("p (m c) -> p m c", c=256))],
                        (lambda ob, ob_b, k=k, mh=mh: hook(k, mh, ob, ob_b)) if hook else None)
    def pp_d(f, hook):
        for m0 in range(0, NM, 2):
            def conv(st, st_b, ob, ob_b, i, m0=m0):
                cvt('act' if (m0 // 2) % 2 == 0 else 'dve', ob[:, 0:2048], st[:, 0:2048], None, [st_b], [ob_b])
            prepass([lambda e, st, m0=m0, f=f: e.dma_start(out=st[:, 0:2048].rearrange("p (m n) -> p m n", m=2),
                                                         in_=wd_d[f][m0 * 128:(m0 + 2) * 128, :].rearrange("(m p) n -> p m n", p=128))],
                    conv,
                    [lambda e, ob, m0=m0, f=f: e.dma_start(out=s_d[f][m0:m0 + 2].rearrange("m p n -> p m n"),
                                                         in_=ob[:, 0:2048].rearrange("p (m n) -> p m n", m=2))],
                    (lambda ob, ob_b, m0=m0: hook(m0, ob, ob_b)) if hook else None)
    def pp_in(hook):
        for k in range(KD):
            col = gcols[:, 8 + k:8 + k + 1]

            def conv(st, st_b, ob, ob_b, i, col=col):
                cvt('act', ob[:, 0:512].rearrange("p (a g d) -> p a g d", a=4, g=2),
                    st[:, 0:512].rearrange("p (g a d) -> p a g d", g=2, a=4), col, [st_b, cst_b], [ob_b])
                cvt('dve', ob[:, 512:768], st[:, 512:768], col, [st_b, cst_b], [ob_b])
                o4 = ob[:, 768:1792].rearrange("p (c t n) -> p c t n", c=4, t=2)
                cvt('act', o4[:, :, 0, :], st[:, 768:1280].rearrange("p (c n) -> p c n", c=4), col, [st_b, cst_b], [ob_b])
                cvt('dve', o4[:, :, 1, :], st[:, 1280:1792].rearrange("p (c n) -> p c n", c=4), col, [st_b, cst_b], [ob_b])
            prepass([lambda e, st, k=k: e.dma_start(out=st[:, 0:1792], in_=win_d[k * 128:(k + 1) * 128, :])],
                    conv,
                    [lambda e, ob, k=k: e.dma_start(out=s_in[:, :, k * 256:(k + 1) * 256].rearrange("m p c -> p m c"),
                                                    in_=ob[:, 0:1792].rearrange("p (m c) -> p m c", c=256))],
                    (lambda ob, ob_b, k=k: hook(k, ob, ob_b)) if hook else None)
    def pp_o(hook):
        for j in range(4):
            kc0 = 2 * j
            srcs = []
            for i2 in range(2):
                kc = kc0 + i2
                if kc < 4:
                    srcs.append(lambda e, st, kc=kc, i2=i2: e.dma_start(out=st[0:64, i2 * 1024:(i2 + 1) * 1024], in_=wout_d[kc * 64:(kc + 1) * 64, :]))
                    srcs.append(lambda e, st, kc=kc, i2=i2: e.dma_start(out=st[64:128, i2 * 1024:(i2 + 1) * 1024], in_=wout_d[(4 + kc) * 64:(5 + kc) * 64, :]))
                else:
                    srcs.append(lambda e, st, kc=kc, i2=i2: e.dma_start(out=st[:, i2 * 1024:(i2 + 1) * 1024], in_=wout_d[512 + (kc - 4) * 128:512 + (kc - 3) * 128, :]))

            def conv(st, st_b, ob, ob_b, i, j=j):
                cvt('act' if j % 2 == 0 else 'dve', ob[:, 0:2048], st[:, 0:2048], None, [st_b], [ob_b])
            prepass(srcs, conv,
                    [lambda e, ob, kc0=kc0: e.dma_start(out=s_o[kc0:kc0 + 2].rearrange("m p n -> p m n"),
                                                        in_=ob[:, 0:2048].rearrange("p (m n) -> p m n", m=2))],
                    (lambda ob, ob_b, j=j: hook(j, ob, ob_b)) if hook else None)

    groups = []
    for b in range(NSEQ):
        for gi in range(NGRP):
            groups.append(dict(kind='p', seq=b, gi=gi, row0=b * SEQ + gi * GT, NT=4, TP=128, N=GT))
    GS = dict(kind='s', seq=0, gi=0, row0=0, NT=1, TP=64, N=64)
    tid = 0
    for G in groups:
        G['slots'] = [(tid + t) % NXS for t in range(G['NT'])]
        tid += G['NT']
    GS['slots'] = [NXS - 1]

    def x_load(G, tiles):
        for t in tiles:
            sl = G['slots'][t]
            TP = G['TP']
            src = (xp if G['kind'] == 'p' else xsm)[G['row0'] + t * TP:G['row0'] + (t + 1) * TP, :]
            S.dma('pool', 'xl%d' % sl, [lambda e: e.dma_start(out=xs[sl][0:TP, :], in_=src)], writes=[xs_b[sl]])

    def x_store(G, t):
        sl = G['slots'][t]
        TP = G['TP']
        dst = (yp if G['kind'] == 'p' else ys)[G['row0'] + t * TP:G['row0'] + (t + 1) * TP, :]
        S.dma('pool', 'xst%d' % sl, [lambda e: e.dma_start(out=dst, in_=xs[sl][0:TP, :])], reads=[xs_b[sl]])

    blocks = []
    for gidx in range(len(groups)):
        for m in range(NM):
            blocks.append(('gu', 0, m))
        for j in range(7):
            blocks.append(('in', 0, j))
        for m in range(NM):
            blocks.append(('gu', 1, m))
    nfill = [0]
    slot_of = {}

    def fill(i):
        kind, f, m = blocks[i]
        rap, rb, rname = ring.next()
        slot_of[i] = (rap, rb)
        src = s_gu[f][m] if kind == 'gu' else s_in[m]
        S.dma('sp', rname, [lambda e: e.dma_start(out=rap.rearrange("p k c -> p (k c)"), in_=src)], writes=[rb])

    def d_load(chunks, src):
        for dm in chunks:
            S.dma('sp', 'D%d' % dm, [lambda e: e.dma_start(out=Dbuf[:, dm, :], in_=src[dm])], writes=[D_b[dm]])

    def get_block(i):
        while nfill[0] <= i + 3 and nfill[0] < len(blocks):
            fill(nfill[0])
            nfill[0] += 1
        return slot_of.pop(i)

    blk = [0]

    def next_block():
        r = get_block(blk[0])
        blk[0] += 1
        return r

    ph = [0]

    def cur_x():
        return xq[ph[0] % 2]

    def nxt_x():
        return xq[(ph[0] + 1) % 2]

    def rms_chain(G, t):
        TP = G['TP']
        sl = G['slots'][t]
        ssq, ssq_b = smr.next()
        act(lambda e: e.activation(out=junk[0:TP, :], in_=xs[sl][0:TP, :], func=AF.Square, accum_out=ssq[0:TP, 0:1]),
            [xs_b[sl], junk_b], [junk_b, ssq_b])
        dve(lambda e: e.tensor_scalar(out=ssq[0:TP, 1:2], in0=ssq[0:TP, 0:1], scalar1=1.0 / D, scalar2=EPS, op0=ALU.mult, op1=ALU.add),
            [ssq_b], [ssq_b])
        pool(lambda e: e.tensor_tensor(out=ssq[0:TP, 2:3], in0=ssq[0:TP, 1:2], in1=nhalf[0:TP, 0:1], op=ALU.pow), [ssq_b, cst_b], [ssq_b])
        xn, xn_b = xn_ring.next()
        dve(lambda e: e.tensor_scalar(out=xn[0:TP, :], in0=xs[sl][0:TP, :], scalar1=ssq[0:TP, 2:3], scalar2=None, op0=ALU.mult),
            [xs_b[sl], ssq_b], [xn_b])
        return (t, xn, xn_b)

    def rms_tr(G, item, dst):
        TP = G['TP']
        t, xn, xn_b = item
        xT, xT_b = dst
        pT, pT_b = pT_ring.next()

        def tr(e):
            ins = None
            for k in range(KD):
                ins = e.transpose(out=pT[:, k * 128:k * 128 + TP], in_=xn[0:TP, k * 128:(k + 1) * 128], identity=identb[0:TP, 0:TP])
            return ins
        pe(tr, [xn_b, cst_b], [pT_b])
        act(lambda e: e.copy(out=xT[:, :, t * TP:(t + 1) * TP], in_=pT.rearrange("p (k n) -> p k n", k=KD)[:, :, 0:TP]),
            [pT_b], [xT_b])

    def ffn(G, f, Gn):
        NT, TP, N = G['NT'], G['TP'], G['N']
        xnT, xnT_b = cur_x()
        gu_pend = []
        for m in range(NM):
            w, w_b = next_block()
            d_load([m], s_d[f])
            pG, pG_b = P_ring.next()
            pU, pU_b = P_ring.next()
            pe(mmgroup(pG[:, 0:N], [(w[:, k, 0:128], xnT[:, k, 0:N]) for k in range(KD)]), [w_b, xnT_b], [pG_b])
            pe(mmgroup(pU[:, 0:N], [(w[:, k, 128:256], xnT[:, k, 0:N]) for k in range(KD)]), [w_b, xnT_b], [pU_b])
            sg, sg_b = tmpf.next()
            act(lambda e: e.activation(out=sg[:, 0:N], in_=pG[:, 0:N], func=AF.Silu), [pG_b], [sg_b])
            dve(lambda e: e.tensor_tensor(out=hT[:, m, 0:N], in0=sg[:, 0:N], in1=pU[:, 0:N], op=ALU.mult), [sg_b, pU_b],
                [hT_bA if m < MSPLIT else hT_bB])
            if f == 1 and Gn is not None:
                for tt in range(Gn['NT']):
                    if m == 4 + 4 * tt:
                        gu_pend.append(rms_chain(Gn, tt))
                    if m == 6 + 4 * tt:
                        rms_tr(Gn, gu_pend.pop(0), nxt_x())
        pend = []
        pendn = []
        done_n = []
        for t in range(NT):
            sl = G['slots'][t]
            for hf in range(2):
                pO, pO_b = P_ring.next()
                pe(mmgroup(pO[0:TP, :], [(hT[:, m, t * TP:(t + 1) * TP], Dbuf[:, m, hf * 512:(hf + 1) * 512]) for m in range(MSPLIT)],
                           first=True, last=False), [hT_bA] + D_b[0:MSPLIT], [pO_b])
                pe(mmgroup(pO[0:TP, :], [(hT[:, m, t * TP:(t + 1) * TP], Dbuf[:, m, hf * 512:(hf + 1) * 512]) for m in range(MSPLIT, NM)],
                           first=False, last=True), [hT_bB] + D_b[MSPLIT:NM], [pO_b])
                xsl = xs[sl][0:TP, hf * 512:(hf + 1) * 512]
                dve(lambda e: e.scalar_tensor_tensor(out=xsl, in0=pO[0:TP, :], scalar=0.5, in1=xsl, op0=ALU.mult, op1=ALU.add),
                    [pO_b, xs_b[sl]], [xs_b[sl]])
            if f == 0:
                pend.append(rms_chain(G, t))
                if len(pend) > 1:
                    rms_tr(G, pend.pop(0), nxt_x())
            else:
                final_norm_store(G, t)
        for it in pend:
            rms_tr(G, it, nxt_x())
        ph[0] += 1

    def final_norm_store(G, t):
        TP = G['TP']
        sl = G['slots'][t]
        ssq, ssq_b = smr.next()
        act(lambda e: e.activation(out=junk[0:TP, :], in_=xs[sl][0:TP, :], func=AF.Square, accum_out=ssq[0:TP, 0:1]),
            [xs_b[sl], junk_b], [junk_b, ssq_b])
        dve(lambda e: e.tensor_scalar(out=ssq[0:TP, 1:2], in0=ssq[0:TP, 0:1], scalar1=1.0 / D, scalar2=EPS, op0=ALU.mult, op1=ALU.add),
            [ssq_b], [ssq_b])
        pool(lambda e: e.tensor_tensor(out=ssq[0:TP, 2:3], in0=ssq[0:TP, 1:2], in1=nhalf[0:TP, 0:1], op=ALU.pow), [ssq_b, cst_b], [ssq_b])
        dve(lambda e: e.scalar_tensor_tensor(out=xs[sl][0:TP, :], in0=xs[sl][0:TP, :], scalar=ssq[0:TP, 2:3], in1=gfin[0:TP, :],
                                             op0=ALU.mult, op1=ALU.mult), [xs_b[sl], ssq_b, gfin_b], [xs_b[sl]])
        x_store(G, t)

    def headnorm_a(pQ, pQ_b, N):
        sq, sq_b = tmpb.next()
        act(lambda e: e.activation(out=sq[:, 0:N], in_=pQ[:, 0:N], func=AF.Square), [pQ_b], [sq_b])
        pS, pS_b = P_ring.next()
        pe(lambda e: e.matmul(pS[:, 0:N], lhsT=blockones, rhs=sq[:, 0:N], start=True, stop=True), [sq_b, cst_b], [pS_b])
        return pS, pS_b

    def headnorm_b(pQ, pQ_b, pS, pS_b, N, gcol, outs):
        r, r_b = tmpf.next()
        act(lambda e: e.activation(out=r[:, 0:N], in_=pS[:, 0:N], func=AF.Ln, scale=1.0 / 64, bias=epsc[:, 0:1]), [pS_b, cst_b], [r_b])
        act(lambda e: e.activation(out=r[:, 0:N], in_=r[:, 0:N], func=AF.Exp, scale=-0.5), [r_b], [r_b])
        for (o, cs, ob) in outs:
            dve(lambda e: e.scalar_tensor_tensor(out=o, in0=pQ[:, cs], scalar=gcol, in1=r[:, cs], op0=ALU.mult, op1=ALU.mult),
                [pQ_b, r_b, cst_b], [ob])

    def mixer(G, pre=None):
        NT, TP, N = G['NT'], G['TP'], G['N']
        prompt = G['kind'] == 'p'
        last = prompt and G['gi'] == NGRP - 1
        first = prompt and G['gi'] == 0
        if pre is None:
            xnT, xnT_b = cur_x()
        kf = kf_b = None
        tiles_qk = []
        for qb in range(2):
            for j in range(2):
                tiles_qk.append(('q', qb, j))
        tiles_qk.append(('k', 2, 0))
        blkw = {}

        def proj_qk(i):
            if pre is not None:
                return pre['qk'][i]
            if i < 4:
                d_load([2 * i, 2 * i + 1], s_o)
            kind, bi, j = tiles_qk[i]
            if bi not in blkw:
                blkw[bi] = next_block()
            w, w_b = blkw[bi]
            pQ, pQ_b = P_ring.next()
            pe(mmgroup(pQ[:, 0:N], [(w[:, k, j * 128:(j + 1) * 128], xnT[:, k, 0:N]) for k in range(KD)]), [w_b, xnT_b], [pQ_b])
            return pQ, pQ_b

        diag = {}

        def build_diags(lo, hi):
            npt = CK - NDVE
            for idx in range(lo, hi):
                c, j = idx // npt, NDVE + idx % npt
                if idx < 112:
                    ch = 8 + idx // 8
                    dg = Dbuf[:, ch, (idx % 8) * 128:(idx % 8 + 1) * 128]
                    dg_b = D_b[ch]
                else:
                    dg, dg_b = dgr.next()
                pool(lambda e: e.tensor_tensor(out=dg, in0=identb, in1=prmT[:, c, j:j + 1].to_broadcast([128, 128]), op=ALU.mult), [cst_b], [dg_b])
                diag[(c, j)] = (dg, dg_b)

        if prompt:
            build_diags(0, CK - NDVE)
        def stage_b(i, pq, pss):
            pQ, pQ_b = pq
            pS, pS_b = pss
            nonlocal_kf = None
            if tiles_qk[i][0] == 'q':
                a = 2 * tiles_qk[i][1] + tiles_qk[i][2]
                headnorm_b(pQ, pQ_b, pS, pS_b, N, qkc[:, 0:1], [(qT[:, a, 0:N], slice(0, N), qT_b)])
            else:
                outs = [(kTe[:, 128:128 + N], slice(0, N), kTe_b)]
                if last or not prompt:
                    kfl = tmpf.next()
                    nk = 128 if prompt else N
                    outs.append((kfl[0][:, 0:nk], slice(N - nk, N), kfl[1]))
                    nonlocal_kf = kfl
                headnorm_b(pQ, pQ_b, pS, pS_b, N, qkc[:, 1:2], outs)
            return nonlocal_kf

        pqs = {}
        pss = {}
        pqs[0] = proj_qk(0)
        pqs[1] = proj_qk(1)
        pss[0] = headnorm_a(pqs[0][0], pqs[0][1], N)
        for i in range(5):
            if i + 2 < 5:
                pqs[i + 2] = proj_qk(i + 2)
            if i + 1 < 5:
                pss[i + 1] = headnorm_a(pqs[i + 1][0], pqs[i + 1][1], N)
            r_ = stage_b(i, pqs[i], pss[i])
            if r_ is not None:
                kf, kf_b = r_
        if prompt:
            build_diags(CK - NDVE, 4 * (CK - NDVE))
        if pre is None:
            w, w_b = blkw[2]
        if kf is not None:
            nk = 128 if prompt else N
            pX, pX_b = PG_ring.next()
            pe(lambda e: e.transpose(out=pX[0:nk, 0:128], in_=kf[:, 0:nk], identity=identf), [kf_b, cst_b], [pX_b])
            ko, ko_b = tmpf.next()
            dve(lambda e: e.tensor_copy(out=ko[0:nk, 0:128], in_=pX[0:nk, 0:128]), [pX_b], [ko_b])
            if prompt:
                S.dma('pool', 'oko', [lambda e: e.dma_start(out=nkp[G['seq']], in_=ko[:, 0:128])], reads=[ko_b])
            else:
                for s_ in range(SB):
                    S.dma('pool', 'oko', [lambda e: e.dma_start(out=nks[s_, R - DS:R, :], in_=ko[s_ * DS:(s_ + 1) * DS, 0:128])], reads=[ko_b])
        if prompt:
            for t in range(NT):
                pV, pV_b = PG_ring.next()
                pe(mmgroup(pV[0:TP, 0:128], [(xnT[:, k, t * TP:(t + 1) * TP], w[:, k, 128:256]) for k in range(KD)]), [w_b, xnT_b], [pV_b])
                act(lambda e: e.copy(out=Vp[:, 1 + t, 0, 0:64], in_=pV[:, 0:64]), [pV_b], [Vp_b])
                dve(lambda e: e.tensor_copy(out=Vp[:, 1 + t, 1, 64:128], in_=pV[:, 64:128]), [pV_b], [Vp_b])
                if last and t == NT - 1:
                    vo, vo_b = tmpf.next()
                    dve(lambda e: e.tensor_copy(out=vo[:, 0:128], in_=pV[:, 0:128]), [pV_b], [vo_b])
                    S.dma('pool', 'ovo', [lambda e: e.dma_start(out=nvp[G['seq']], in_=vo[:, 0:128])], reads=[vo_b])
        elif pre is None:
            for s_ in range(SB):
                pV, pV_b = PG_ring.next()
                pe(mmgroup(pV[0:DS, 0:128], [(xnT[:, k, s_ * DS:(s_ + 1) * DS], w[:, k, 128:256]) for k in range(KD)]), [w_b, xnT_b], [pV_b])
                act(lambda e: e.copy(out=Vsn[:, s_, 0, 0:64], in_=pV[0:DS, 0:64]), [pV_b], [smp_b])
                dve(lambda e: e.tensor_copy(out=Vsn[:, s_, 1, 64:128], in_=pV[0:DS, 64:128]), [pV_b], [smp_b])
                vo, vo_b = tmpf.next()
                dve(lambda e: e.tensor_copy(out=vo[0:DS, 0:128], in_=pV[0:DS, 0:128]), [pV_b], [vo_b])
                S.dma('pool', 'ovo%d' % s_, [lambda e: e.dma_start(out=nvs[s_, R - DS:R, :], in_=vo[0:DS, 0:128])], reads=[vo_b])
        if first:
            dve(lambda e: e.memset(uext[:, :, 0:CK - 1], 0.0), [], [uext_b])
        ub = uext_b if prompt else smp_b

        def usl(c, j):
            return uexs[:, c, :, j:j + DS]

        def asl(c):
            return acc[:, c, 0:N].rearrange("p (s i) -> p s i", s=SB)

        if prompt:
            units = [dict(q0=p * 128, nq=128, tiles=[('A', kTe[:, p * 128:(p + 1) * 128], Vp[:, p], 0, 128),
                                                    ('B', kTe[:, (p + 1) * 128:(p + 2) * 128], Vp[:, p + 1], 1, 128)][(1 if (first and p == 0) else 0):])
                     for p in range(4)]
        else:
            units = [dict(q0=s_ * DS, nq=DS, tiles=[('A', kTc[:, s_, :], Vpc[:, s_], 0, 128),
                                                    ('B', kTe[:, 128 + s_ * DS:128 + (s_ + 1) * DS], Vsn[:, s_], 1, DS)])
                     for s_ in range(SB)]
        kvb = [kTe_b, Vp_b, smp_b, qT_b, ebias_b, cst_b]
        (pO1, pO1_b), (pO2, pO2_b) = PO

        def att_part1(U):
            q0, nq = U['q0'], U['nq']
            NQ = 4 * nq
            U['steps'] = []
            for g in range(2):
                for (nm, kap, vap, kt, nkeys) in U['tiles']:
                    gs = slice(64 * g, 64 * g + 64)
                    pSc, pSc_b = PS_ring.next()
                    sc_o = pSc[0:nkeys, 0:NQ].rearrange("p (a n) -> p a n", a=4)
                    def scfn(e):
                        e.matmul(sc_o, lhsT=kap[gs, :], rhs=qT[gs, :, q0:q0 + nq], start=True, stop=False)
                        return e.matmul(sc_o, lhsT=identb[0:nkeys, 0:nkeys], rhs=ebias[g][kt][0:nkeys, :, 0:nq], start=False, stop=True)
                    pe(scfn, kvb, [pSc_b])
                    et, et_b = tmpb.next()
                    act(lambda e: e.activation(out=et[0:nkeys, 0:NQ], in_=pSc[0:nkeys, 0:NQ], func=AF.Exp), [pSc_b], [et_b])
                    U['steps'].append((g, vap, nkeys, et, et_b))

        def att_part2(U):
            q0, nq = U['q0'], U['nq']
            NQ = 4 * nq
            if prompt:
                o1 = pO1[:, 0:NQ].rearrange("p (a n) -> p a n", a=4)
                o2 = pO2[:, 0:NQ].rearrange("p (a n) -> p a n", a=4)
            else:
                o1 = pO1[:, 0:256].rearrange("p (a n) -> p a n", a=4)[:, :, q0:q0 + nq]
                o2 = pO2[:, 0:256].rearrange("p (a n) -> p a n", a=4)[:, :, q0:q0 + nq]
            ns = len(U['steps'])
            for si, (g, vap, nkeys, et, et_b) in enumerate(U['steps']):
                etv = et[0:nkeys, 0:NQ].rearrange("p (a n) -> p a n", a=4)
                st_, sp_ = (si == 0), (si == ns - 1)
                def pvfn(e):
                    e.matmul(o1, lhsT=vap[0:nkeys, g, :], rhs=etv, start=st_, stop=sp_)
                    return e.matmul(o2, lhsT=onespad[0:nkeys, g, :], rhs=etv, start=st_, stop=sp_)
                pe(pvfn, [et_b] + kvb, [pO1_b, pO2_b])
            if prompt:
                den, den_b = tmpf.next()
                dve(lambda e: e.tensor_tensor(out=den, in0=pO2, in1=sinkexp.rearrange("p a n -> p (a n)"), op=ALU.add), [pO2_b, cst_b], [den_b])
                act(lambda e: e.activation(out=den, in_=den, func=AF.Ln), [den_b], [den_b])
                act(lambda e: e.activation(out=den, in_=den, func=AF.Exp, scale=-1.0), [den_b], [den_b])
                dve(lambda e: e.tensor_tensor(out=catT[:, 0:4, q0:q0 + 128], in0=pO1.rearrange("p (a n) -> p a n", a=4),
                                              in1=den.rearrange("p (a n) -> p a n", a=4), op=ALU.mult), [pO1_b, den_b], [catA_b])

        def glu(c):
            if pre is not None:
                pA, pA_b = pre['A'][c]
                pGg, pGg_b = pre['G'][c]
            else:
                w, w_b = next_block()
                pA, pA_b = PG_ring.next()
                pGg, pGg_b = PG_ring.next()
                pe(mmgroup(pA[:, 0:N], [(w[:, k, 0:128], xnT[:, k, 0:N]) for k in range(KD)]), [w_b, xnT_b], [pA_b])
                pe(mmgroup(pGg[:, 0:N], [(w[:, k, 128:256], xnT[:, k, 0:N]) for k in range(KD)]), [w_b, xnT_b], [pGg_b])
            th, th_b = tmpf.next()
            act(lambda e: e.activation(out=th[:, 0:N], in_=pGg[:, 0:N], func=AF.Exp, scale=-1.0), [pGg_b], [th_b])
            act(lambda e: e.activation(out=th[:, 0:N], in_=th[:, 0:N], func=AF.Ln, bias=onec[:, 0:1]), [th_b, cst_b], [th_b])
            act(lambda e: e.activation(out=th[:, 0:N], in_=th[:, 0:N], func=AF.Exp, scale=-1.0), [th_b], [th_b])
            if prompt:
                dve(lambda e: e.tensor_tensor(out=uext[:, c, CK - 1:CK - 1 + N], in0=pA[:, 0:N], in1=th[:, 0:N], op=ALU.mult),
                    [pA_b, th_b], [uext_b])
                if last:
                    dve(lambda e: e.tensor_tensor(out=utail[:, c, 0:CK - 1], in0=pA[:, N - (CK - 1):N], in1=th[:, N - (CK - 1):N], op=ALU.mult),
                        [pA_b, th_b], [utail_b])
                dve(lambda e: e.tensor_scalar(out=acc[:, c, 0:N], in0=uext[:, c, 0:N], scalar1=prmT[:, c, 0:1], scalar2=prmT[:, c, 31:32],
                                              op0=ALU.mult, op1=ALU.add), [uext_b, cst_b], [acc_cb[c]])
                for j in range(1, NDVE):
                    dve(lambda e: e.scalar_tensor_tensor(out=acc[:, c, 0:N], in0=uext[:, c, j:j + N], scalar=prmT[:, c, j:j + 1], in1=acc[:, c, 0:N],
                                                         op0=ALU.mult, op1=ALU.add), [uext_b, cst_b, acc_cb[c]], [acc_cb[c]])
            else:
                dve(lambda e: e.tensor_tensor(out=uexs[:, c, :, CK - 1:CK - 1 + DS], in0=pA[:, 0:N].rearrange("p (s i) -> p s i", s=SB),
                                              in1=th[:, 0:N].rearrange("p (s i) -> p s i", s=SB), op=ALU.mult), [pA_b, th_b], [smp_b])

        glu(0)
        for c in range(4):
            if c + 1 < 4:
                glu(c + 1)
            att_part1(units[c])
            if prompt:
                pc, pc_b = PG_ring.next()
                dbs = []
                for j in range(NDVE, CK):
                    if diag[(c, j)][1] not in dbs:
                        dbs.append(diag[(c, j)][1])
                pe(mmgroup(pc[:, 0:N], [(diag[(c, j)][0], uext[:, c, j:j + N]) for j in range(NDVE, CK)]), dbs + [uext_b], [pc_b])
                dve(lambda e: e.tensor_tensor(out=acc[:, c, 0:N], in0=pc[:, 0:N], in1=acc[:, c, 0:N], op=ALU.add),
                    [pc_b, acc_cb[c]], [acc_cb[c]])
            elif c == 3:
                for cc in range(4):
                    dve(lambda e: e.tensor_scalar(out=asl(cc), in0=usl(cc, 0), scalar1=prmT[:, cc, 0:1], scalar2=prmT[:, cc, 31:32],
                                                  op0=ALU.mult, op1=ALU.add), [ub, cst_b, acc_b], [acc_cb[cc]])
                for j in range(1, CK):
                    for cc in range(4):
                        dve(lambda e: e.scalar_tensor_tensor(out=asl(cc), in0=usl(cc, j), scalar=prmT[:, cc, j:j + 1], in1=asl(cc),
                                                             op0=ALU.mult, op1=ALU.add), [ub, cst_b, acc_cb[cc]], [acc_cb[cc]])
            att_part2(units[c])
        if not prompt:
            den, den_b = tmpf.next()
            d3 = den[:, 0:256].rearrange("p (a n) -> p a n", a=4)
            dve(lambda e: e.tensor_tensor(out=d3, in0=pO2[:, 0:256].rearrange("p (a n) -> p a n", a=4), in1=sinkexp[:, :, 0:64], op=ALU.add),
                [pO2_b, cst_b], [den_b])
            act(lambda e: e.activation(out=den[:, 0:256], in_=den[:, 0:256], func=AF.Ln), [den_b], [den_b])
            act(lambda e: e.activation(out=den[:, 0:256], in_=den[:, 0:256], func=AF.Exp, scale=-1.0), [den_b], [den_b])
            dve(lambda e: e.tensor_tensor(out=catT[:, 0:4, 0:64], in0=pO1[:, 0:256].rearrange("p (a n) -> p a n", a=4), in1=d3, op=ALU.mult),
                [pO1_b, den_b], [catA_b])
        if prompt and last:
            pX, pX_b = PG_ring.next()
            for c in range(4):
                pe(lambda e: e.transpose(out=pX[0:CK - 1, c * 128:(c + 1) * 128], in_=utail[:, c, 0:CK - 1], identity=identf),
                   [utail_b, cst_b], [pX_b])
            uo, uo_b = tmpf.next()
            dve(lambda e: e.tensor_copy(out=uo[0:CK - 1, :], in_=pX[0:CK - 1, :]), [pX_b], [uo_b])
            S.dma('pool', 'ouo', [lambda e: e.dma_start(out=ncp[G['seq']], in_=uo[0:CK - 1, :])], reads=[uo_b])
        elif prompt:
            act(lambda e: e.copy(out=uext[:, :, 0:CK - 1], in_=uext[:, :, GT:GT + CK - 1]), [uext_b], [uext_b])
        else:
            for s_ in range(SB):
                pX, pX_b = PG_ring.next()
                for c in range(4):
                    pe(lambda e: e.transpose(out=pX[0:DS, c * 128:(c + 1) * 128], in_=uexs[:, c, s_, CK - 1:CK - 1 + DS], identity=identf),
                       [smp_b, cst_b], [pX_b])
                uo, uo_b = tmpf.next()
                dve(lambda e: e.tensor_copy(out=uo[0:DS, :], in_=pX[0:DS, :]), [pX_b], [uo_b])
                S.dma('pool', 'ouo%d' % s_, [lambda e: e.dma_start(out=ncs[s_, CK - 1 - DS:CK - 1, :], in_=uo[0:DS, :])], reads=[uo_b])
        if prompt and not last:
            act(lambda e: e.copy(out=kTe[:, 0:128], in_=kTe[:, GT:GT + 128]), [kTe_b], [kTe_b])
            dve(lambda e: e.tensor_copy(out=Vp[:, 0], in_=Vp[:, 4]), [Vp_b], [Vp_b])
        pM, pM_b = PG_ring.next()
        pV2, pV2_b = PG_ring.next()
        pe(mmgroup(pM[:, 0:N], [(onesf, acc[:, c, 0:N]) for c in range(4)]), [acc_b, cst_b] + acc_cb, [pM_b])
        ysqs = []
        for c in range(4):
            ysq, ysq_b = tmpf.next()
            act(lambda e: e.activation(out=ysq[:, 0:N], in_=acc[:, c, 0:N], func=AF.Square), [acc_b, acc_cb[c]], [ysq_b])
            ysqs.append((ysq, ysq_b))
        pe(mmgroup(pV2[:, 0:N], [(onesf, y_[:, 0:N]) for (y_, _) in ysqs]), [b_ for (_, b_) in ysqs] + [cst_b], [pV2_b])
        pgx = PG_ring.next()
        obanks = [(PS_ring.items[0], PS_ring.items[1]), (PO[0], PO[1]), (PS_ring.items[2], pgx)]
        n_early = min(NT, 3) if pre is None else 0
        for t in range(n_early):
            for hf in range(2):
                pO, pO_b = obanks[t][hf]
                pe(mmgroup(pO[0:TP, :], [(catT[:, kc, t * TP:(t + 1) * TP], Dbuf[:, kc, hf * 512:(hf + 1) * 512]) for kc in range(4)],
                           first=True, last=False), [catA_b] + D_b[0:4], [pO_b])
        mean, mean_b = tmpf.next()
        dve(lambda e: e.tensor_scalar(out=mean[:, 0:N], in0=pM[:, 0:N], scalar1=1.0 / 512, scalar2=None, op0=ALU.mult), [pM_b], [mean_b])
        var, var_b = tmpf.next()
        dve(lambda e: e.tensor_tensor(out=var[:, 0:N], in0=mean[:, 0:N], in1=mean[:, 0:N], op=ALU.mult), [mean_b], [var_b])
        dve(lambda e: e.scalar_tensor_tensor(out=var[:, 0:N], in0=pV2[:, 0:N], scalar=1.0 / 512, in1=var[:, 0:N], op0=ALU.mult, op1=ALU.subtract),
            [pV2_b, var_b], [var_b])
        rstd, rstd_b = tmpf.next()
        act(lambda e: e.activation(out=rstd[:, 0:N], in_=var[:, 0:N], func=AF.Ln, bias=epsc[:, 0:1]), [var_b, cst_b], [rstd_b])
        act(lambda e: e.activation(out=rstd[:, 0:N], in_=rstd[:, 0:N], func=AF.Exp, scale=-0.5), [rstd_b], [rstd_b])
        for c in range(4):
            t1, t1_b = tmpf.next()
            dve(lambda e: e.tensor_tensor(out=t1[:, 0:N], in0=acc[:, c, 0:N], in1=mean[:, 0:N], op=ALU.subtract), [acc_b, acc_cb[c], mean_b], [t1_b])
            dve(lambda e: e.tensor_tensor(out=t1[:, 0:N], in0=t1[:, 0:N], in1=rstd[:, 0:N], op=ALU.mult), [t1_b, rstd_b], [t1_b])
            act(lambda e: e.activation(out=catT[:, 4 + c, 0:N], in_=t1[:, 0:N], func=AF.Silu, scale=prmT[:, c, 32:33], bias=prmT[:, c, 33:34]),
                [t1_b, cst_b], [catC_b])
        if pre is not None:
            return
        pend = []
        for t in range(NT):
            sl = G['slots'][t]
            for hf in range(2):
                if t < n_early:
                    pO, pO_b = obanks[t][hf]
                    pe(mmgroup(pO[0:TP, :], [(catT[:, kc, t * TP:(t + 1) * TP], Dbuf[:, kc, hf * 512:(hf + 1) * 512]) for kc in range(4, KD)],
                               first=False, last=True), [catC_b] + D_b[4:KD], [pO_b])
                else:
                    pO, pO_b = PG_ring.next()
                    pe(mmgroup(pO[0:TP, :], [(catT[:, kc, t * TP:(t + 1) * TP], Dbuf[:, kc, hf * 512:(hf + 1) * 512]) for kc in range(KD)]),
                       [catA_b, catC_b] + D_b[0:KD], [pO_b])
                xsl = xs[sl][0:TP, hf * 512:(hf + 1) * 512]
                dve(lambda e: e.tensor_tensor(out=xsl, in0=pO[0:TP, :], in1=xsl, op=ALU.add), [pO_b, xs_b[sl]], [xs_b[sl]])
            pend.append(rms_chain(G, t))
            if len(pend) > 1:
                rms_tr(G, pend.pop(0), nxt_x())
        for it in pend:
            rms_tr(G, it, nxt_x())
        ph[0] += 1

    def sample_prep():
        st, st_b = tmpf.next()
        S.dma('pool', 'smp0', [lambda e: e.dma_start(out=st.rearrange("p (s c) -> p s c", s=SB), in_=ck.rearrange("s r c -> r s c"))], writes=[st_b])
        cb, cb_b = tmpb.next()
        dve(lambda e: e.tensor_copy(out=cb, in_=st), [st_b], [cb_b])
        pT, pT_b = pT_ring.next()

        def tr(e):
            ins = None
            for s_ in range(SB):
                ins = e.transpose(out=pT[:, s_ * 128:(s_ + 1) * 128], in_=cb[:, s_ * 128:(s_ + 1) * 128], identity=identb)
            return ins
        pe(tr, [cb_b, cst_b], [pT_b])
        act(lambda e: e.copy(out=kTc, in_=pT[:, 0:512].rearrange("p (s n) -> p s n", s=SB)), [pT_b], [smp_b])
        st2, st2_b = tmpf.next()
        S.dma('pool', 'smp1', [lambda e: e.dma_start(out=st2.rearrange("p (s c) -> p s c", s=SB), in_=cv.rearrange("s r c -> r s c"))], writes=[st2_b])
        s3 = st2.rearrange("p (s c) -> p s c", s=SB)
        dve(lambda e: e.tensor_copy(out=Vpc[:, :, 0, 0:64], in_=s3[:, :, 0:64]), [st2_b], [smp_b])
        dve(lambda e: e.tensor_copy(out=Vpc[:, :, 1, 64:128], in_=s3[:, :, 64:128]), [st2_b], [smp_b])
        stc = acc[0:CK - 1, :, :]
        S.dma('pool', 'smp2', [lambda e: e.dma_start(out=stc, in_=sc.rearrange("s r c -> r s c"))], writes=[acc_b])
        pX, pX_b = P_ring.next()
        for c in range(4):
            for s_ in range(SB):
                o = pX[:, (c * SB + s_) * (CK - 1):(c * SB + s_ + 1) * (CK - 1)]
                pe(lambda e: e.transpose(out=o, in_=stc[:, s_, c * 128:(c + 1) * 128], identity=identf[0:CK - 1, 0:CK - 1]), [acc_b, cst_b], [pX_b])
        dve(lambda e: e.tensor_copy(out=uexs[:, :, :, 0:CK - 1], in_=pX[:, 0:16 * (CK - 1)].rearrange("p (c s j) -> p c s j", c=4, s=SB)),
            [pX_b], [smp_b])
        S.dma('pool', 'd2d', [lambda e: e.dma_start(out=nks[:, 0:R - DS, :], in_=ck[:, DS:R, :]),
                              lambda e: e.dma_start(out=nvs[:, 0:R - DS, :], in_=cv[:, DS:R, :]),
                              lambda e: e.dma_start(out=ncs[:, 0:CK - 1 - DS, :], in_=sc[:, DS:CK - 1, :])])

    uflat = uext.rearrange("p c n -> p (c n)").bitcast(F32)
    sproj = uflat[:, 0:13 * 64]
    sproj_b = Buf("sproj")
    Pb = [it[1] for it in P_items]

    def gacc(m, up):
        return P_items[(3 if up else 0) + m // 8][0][:, (m % 8) * 64:(m % 8) * 64 + 64]

    def mk_gu_hook(f, xin):
        def hook(k, mh, ob, ob_b):
            o3 = ob.rearrange("p (m c) -> p m c", c=256)

            def fn(e):
                ins = None
                for mi in range(11):
                    m = mh * 11 + mi
                    st_ = (k == 0 and m % 8 == 0)
                    e.matmul(gacc(m, False), lhsT=o3[:, mi, 0:128], rhs=xin[0][:, k, 0:64], start=st_, stop=(k == KD - 1), skip_group_check=True)
                    ins = e.matmul(gacc(m, True), lhsT=o3[:, mi, 128:256], rhs=xin[0][:, k, 0:64], start=st_, stop=(k == KD - 1), skip_group_check=True)
                return ins
            pe(fn, [ob_b, xin[1]], Pb)
            if k == KD - 1 and mh == 1:
                for b_ in range(3):
                    nm = 8 if b_ < 2 else NM - 16
                    sg, sg_b = tmpf.next()
                    act(lambda e: e.activation(out=sg[:, 0:nm * 64], in_=P_items[b_][0][:, 0:nm * 64], func=AF.Silu), [Pb[b_]], [sg_b])
                    dve(lambda e: e.tensor_tensor(out=hT[:, 8 * b_:8 * b_ + nm, 0:64], in0=sg[:, 0:nm * 64].rearrange("p (m n) -> p m n", n=64),
                                                  in1=P_items[3 + b_][0][:, 0:nm * 64].rearrange("p (m n) -> p m n", n=64), op=ALU.mult),
                        [sg_b, Pb[3 + b_]], [hT_bA, hT_bB])
        return hook

    def mk_d_hook(f):
        def hook(m0, ob, ob_b):
            def fn(e):
                ins = None
                for mi in range(2):
                    m = m0 + mi
                    for hf in range(2):
                        ins = e.matmul(P_items[hf][0][0:64, :], lhsT=hT[:, m, 0:64], rhs=ob[:, mi * 1024 + hf * 512:mi * 1024 + (hf + 1) * 512],
                                       start=(m == 0), stop=(m == NM - 1))
                return ins
            pe(fn, [ob_b, hT_bA, hT_bB], [Pb[0], Pb[1]])
            if m0 == NM - 2:
                sl = GS['slots'][0]
                for hf in range(2):
                    xsl = xs[sl][0:64, hf * 512:(hf + 1) * 512]
                    dve(lambda e: e.scalar_tensor_tensor(out=xsl, in0=P_items[hf][0][0:64, :], scalar=0.5, in1=xsl, op0=ALU.mult, op1=ALU.add),
                        [Pb[hf], xs_b[sl]], [xs_b[sl]])
                if f == 0:
                    rms_tr(GS, rms_chain(GS, 0), xq[1])
                else:
                    final_norm_store(GS, 0)
        return hook

    pTf = [(pT_ring.items[i][0].bitcast(F32), pT_ring.items[i][1]) for i in range(2)]

    def facc(i):
        return pTf[i // 8][0][:, (i % 8) * 64:(i % 8) * 64 + 64]

    in_cols = [0, 128, 256, 384, 512] + [768 + 256 * c + 128 * t for c in range(4) for t in range(2)]

    def in_hook(k, ob, ob_b):
        xin = xq[1]

        def fn(e):
            ins = None
            for i, c0 in enumerate(in_cols):
                ins = e.matmul(facc(i), lhsT=ob[:, c0:c0 + 128], rhs=xin[0][:, k, 0:64], start=(k == 0 and i % 8 == 0), stop=(k == KD - 1),
                               skip_group_check=True)
            for s_ in range(SB):
                ins = e.matmul(P_items[5][0][0:DS, s_ * 128:(s_ + 1) * 128], lhsT=xin[0][:, k, s_ * DS:(s_ + 1) * DS], rhs=ob[:, 640:768],
                               start=(k == 0 and s_ == 0), stop=(k == KD - 1), skip_group_check=True)
            return ins
        pe(fn, [ob_b, xin[1]], [pTf[0][1], pTf[1][1], Pb[5]])
        if k == KD - 1:
            act(lambda e: e.copy(out=sproj[:, 0:512], in_=pTf[0][0]), [pTf[0][1]], [sproj_b])
            act(lambda e: e.copy(out=sproj[:, 512:832], in_=pTf[1][0][:, 0:320]), [pTf[1][1]], [sproj_b])
            pV = P_items[5][0]
            for s_ in range(SB):
                act(lambda e: e.copy(out=Vsn[:, s_, 0, 0:64], in_=pV[0:DS, s_ * 128:s_ * 128 + 64]), [Pb[5]], [smp_b])
                dve(lambda e: e.tensor_copy(out=Vsn[:, s_, 1, 64:128], in_=pV[0:DS, s_ * 128 + 64:(s_ + 1) * 128]), [Pb[5]], [smp_b])
                vo, vo_b = tmpf.next()
                dve(lambda e: e.tensor_copy(out=vo[0:DS, 0:128], in_=pV[0:DS, s_ * 128:(s_ + 1) * 128]), [Pb[5]], [vo_b])
                S.dma('pool', 'ovo%d' % s_, [lambda e: e.dma_start(out=nvs[s_, R - DS:R, :], in_=vo[0:DS, 0:128])], reads=[vo_b])
            pre = dict(qk=[(sproj[:, i * 64:(i + 1) * 64], sproj_b) for i in range(5)],
                       A=[(sproj[:, (5 + 2 * c) * 64:(6 + 2 * c) * 64], sproj_b) for c in range(4)],
                       G=[(sproj[:, (6 + 2 * c) * 64:(7 + 2 * c) * 64], sproj_b) for c in range(4)])
            mixer(GS, pre)

    def o_hook(j, ob, ob_b):
        def fn(e):
            ins = None
            for i2 in range(2):
                kc = 2 * j + i2
                for hf in range(2):
                    ins = e.matmul(P_items[hf][0][0:64, :], lhsT=catT[:, kc, 0:64], rhs=ob[:, i2 * 1024 + hf * 512:i2 * 1024 + (hf + 1) * 512],
                                   start=(kc == 0), stop=(kc == KD - 1))
            return ins
        pe(fn, [ob_b, catA_b, catC_b], [Pb[0], Pb[1]])
        if j == 3:
            sl = GS['slots'][0]
            for hf in range(2):
                xsl = xs[sl][0:64, hf * 512:(hf + 1) * 512]
                dve(lambda e: e.tensor_tensor(out=xsl, in0=P_items[hf][0][0:64, :], in1=xsl, op=ALU.add), [Pb[hf], xs_b[sl]], [xs_b[sl]])
            rms_tr(GS, rms_chain(GS, 0), xq[0])

    S.barrier()
    x_load(GS, [0])
    sample_prep()
    rms_tr(GS, rms_chain(GS, 0), xq[0])
    pp_gu(0, mk_gu_hook(0, xq[0]))
    pp_d(0, mk_d_hook(0))
    pp_in(in_hook)
    pp_o(o_hook)
    pp_gu(1, mk_gu_hook(1, xq[0]))
    pp_d(1, mk_d_hook(1))
    S.barrier()
    x_load(groups[0], range(groups[0]['NT']))
    for t in range(groups[0]['NT']):
        rms_tr(groups[0], rms_chain(groups[0], t), cur_x())
    for gidx, G in enumerate(groups):
        Gn = groups[gidx + 1] if gidx + 1 < len(groups) else None
        if Gn is not None:
            x_load(Gn, range(Gn['NT']))
        ffn(G, 0, Gn)
        mixer(G)
        ffn(G, 1, Gn)
    S.finish('sp')
    return nc


def _rel_bucket_np(rel):
    rel = np.asarray(rel, dtype=np.int64)
    nb = 16
    max_exact = 8
    ret = np.where(rel > 0, nb, 0)
    n = np.abs(rel)
    nf = np.maximum(n, 1).astype(np.float32)
    large = max_exact + (np.log(nf / np.float32(max_exact)) / np.float32(np.log(128 / max_exact))
                         * np.float32(nb - max_exact)).astype(np.int32)
    large = np.minimum(large, nb - 1)
    return ret + np.where(n < max_exact, n, large)


_NC_CACHE = {}


def kernel(x_prompt, x_sample, cache_k, cache_v, state_conv, rel_bias_table,
           ffn1_norm, ffn1_w_gu, ffn1_w_down, mix_norm, w_in, q_norm, k_norm, sinks,
           conv_w, conv_b, conv_ln_g, conv_ln_b, w_out, ffn2_norm, ffn2_w_gu,
           ffn2_w_down, final_norm):
    f = lambda a: np.ascontiguousarray(np.asarray(a, dtype=np.float32))
    if "nc" not in _NC_CACHE:
        _NC_CACHE["nc"] = build_nc()
    nc = _NC_CACHE["nc"]
    ident = np.eye(128, dtype=np.float32)
    jmat = np.ascontiguousarray(ident[::-1])
    j = np.arange(512)
    bk = _rel_bucket_np(255 - j)
    ohr = np.zeros((32, 512), np.float32)
    ohr[bk, j] = 1.0
    shared = dict(
        rbt=f(rel_bias_table), ffn1_norm=f(ffn1_norm[0:1]), mix_norm=f(mix_norm[0:1]), ffn2_norm=f(ffn2_norm[0:1]),
        ffn1_w_gu=f(ffn1_w_gu[0]), ffn2_w_gu=f(ffn2_w_gu[0]), ffn1_w_down=f(ffn1_w_down[0]), ffn2_w_down=f(ffn2_w_down[0]),
        w_in=f(w_in[0]), q_norm=f(q_norm[0:1]), k_norm=f(k_norm[0:1]), sinks=f(sinks[0:1]), conv_w=f(conv_w[0]),
        conv_b=f(conv_b[0:1]), conv_ln_g=f(conv_ln_g[0:1]), conv_ln_b=f(conv_ln_b[0:1]), w_out=f(w_out[0]),
        final_norm=f(final_norm[0:1]), ident=ident, jmat=jmat, ohr=ohr)
    xpf, xsf = f(x_prompt), f(x_sample)
    ckf, cvf, scf = f(cache_k), f(cache_v), f(state_conv)
    in_maps = []
    for c in range(NCORES):
        m = dict(shared)
        m["xp"] = xpf[NSEQ * c:NSEQ * (c + 1)].reshape(NSEQ * SEQ, D)
        m["xsm"] = xsf[SB * c:SB * (c + 1)].reshape(SB * DS, D)
        m["ck"] = ckf[0, SB * c:SB * (c + 1)].reshape(SB, R, 128)
        m["cv"] = cvf[0, SB * c:SB * (c + 1)].reshape(SB, R, 128)
        m["sc"] = scf[0, SB * c:SB * (c + 1)]
        in_maps.append(m)
    res = run_bass_kernel_spmd(nc, in_maps, core_ids=list(range(NCORES)))
    rs = res.results
    cat = lambda k: np.concatenate([np.asarray(r[k], dtype=np.float32) for r in rs], axis=0)
    y_p = cat("yp").reshape(16, SEQ, D)
    y_s = cat("ys").reshape(32, DS, D)
    nkp = cat("nkp").reshape(1, 16, R, 2, 64)
    nvp = cat("nvp").reshape(1, 16, R, 2, 64)
    ncp = cat("ncp").reshape(1, 16, CK - 1, 512)
    nks = cat("nks").reshape(1, 32, R, 2, 64)
    nvs = cat("nvs").reshape(1, 32, R, 2, 64)
    ncs = cat("ncs").reshape(1, 32, CK - 1, 512)
    return (y_p, y_s, nkp, nvp, ncp, nks, nvs, ncs)
```

```python
import numpy as np
import concourse.bass as bass
import concourse.mybir as mybir
from concourse.bass_utils import run_bass_kernel_spmd

F32 = mybir.dt.float32
BF16 = mybir.dt.bfloat16
AF = mybir.ActivationFunctionType
ALU = mybir.AluOpType

NCORES = 8
D = 1024
KD = 8
DFF = 2816
NM = 22
SEQ = 4096
NSEQ = 2
GT = 512
NGRP = SEQ // GT
SB = 4
DS = 16
R = 128
CK = 31
EPS = 1e-6
NXS = 8
NDVE = 8
STRICT = False


class Buf:
    __slots__ = ("name", "w", "r")

    def __init__(self, name):
        self.name = name
        self.w = None
        self.r = {}


class Ring:
    def __init__(self, items):
        self.items = items
        self.i = 0

    def next(self):
        it = self.items[self.i % len(self.items)]
        self.i += 1
        return it


class Sched:
    def __init__(self, nc):
        self.nc = nc
        self.eng = {'pe': nc.tensor, 'act': nc.scalar, 'dve': nc.vector, 'pool': nc.gpsimd, 'sp': nc.sync}
        self.sem = {e: nc.alloc_semaphore("sem_" + e) for e in self.eng}
        self.cnt = {e: 0 for e in self.eng}
        self.known = {e: {} for e in self.eng}
        self.dsem = {}
        self.dcnt = {}

    def _wait(self, e, toks):
        need = {}
        for (k, s, v) in toks:
            if k not in need or need[k][1] < v:
                need[k] = (s, v)
        for k, (s, v) in need.items():
            if self.known[e].get(k, 0) >= v:
                continue
            self.eng[e].wait_ge(s, v)
            self.known[e][k] = v

    def _deps(self, e, reads, writes, skip_waw=False):
        toks = []
        own = "sem_" + e
        for b in reads:
            if b.w is not None:
                toks.append(b.w)
        for b in writes:
            if b.w is not None and (STRICT or b.w[0] != own) and not skip_waw:
                toks.append(b.w)
            for k, (s, v) in b.r.items():
                if STRICT or k != own:
                    toks.append((k, s, v))
        return toks

    def _mark(self, tok, reads, writes):
        for b in reads:
            b.r[tok[0]] = (tok[1], tok[2])
        for b in writes:
            b.w = tok
            b.r = {}

    def op(self, e, fn, reads=(), writes=()):
        self._wait(e, self._deps(e, reads, writes))
        ins = fn(self.eng[e])
        self.cnt[e] += 1
        ins.then_inc(self.sem[e], 1)
        tok = ("sem_" + e, self.sem[e], self.cnt[e])
        self._mark(tok, reads, writes)
        return tok

    def dma(self, q, semname, fns, reads=(), writes=(), skip_waw=False):
        if semname not in self.dsem:
            self.dsem[semname] = self.nc.alloc_semaphore("ds_" + semname)
            self.dcnt[semname] = 0
        s = self.dsem[semname]
        self._wait(q, self._deps(q, reads, writes, skip_waw))
        for fn in fns:
            ins = fn(self.eng[q])
            self.dcnt[semname] += 16
            ins.then_inc(s, 16)
        tok = ("ds_" + semname, s, self.dcnt[semname])
        self._mark(tok, reads, writes)
        return tok

    def barrier(self):
        for e in self.eng:
            for name, s in self.dsem.items():
                if self.dcnt[name] > self.known[e].get("ds_" + name, 0):
                    self.eng[e].wait_ge(s, self.dcnt[name])
                    self.known[e]["ds_" + name] = self.dcnt[name]
            for en in self.eng:
                if en != e and self.cnt[en] > self.known[e].get("sem_" + en, 0):
                    self.eng[e].wait_ge(self.sem[en], self.cnt[en])
                    self.known[e]["sem_" + en] = self.cnt[en]

    def finish(self, e='sp'):
        for name, s in self.dsem.items():
            if self.dcnt[name] > 0:
                self.eng[e].wait_ge(s, self.dcnt[name])
        for en in self.eng:
            if en != e and self.cnt[en] > 0:
                self.eng[e].wait_ge(self.sem[en], self.cnt[en])


def build_nc():
    nc = bass.Bass("TRN2", target_bir_lowering=False)
    S = Sched(nc)

    def din(name, shape, dt=F32):
        return nc.dram_tensor(name, list(shape), dt, kind="ExternalInput").ap()

    def dout(name, shape):
        return nc.dram_tensor(name, list(shape), F32, kind="ExternalOutput").ap()

    xp = din("xp", [NSEQ * SEQ, D])
    xsm = din("xsm", [SB * DS, D])
    ck = din("ck", [SB, R, 128])
    cv = din("cv", [SB, R, 128])
    sc = din("sc", [SB, CK - 1, 512])
    rbt = din("rbt", [32, 8])
    gains_d = [din("ffn1_norm", [1, D]), din("mix_norm", [1, D]), din("ffn2_norm", [1, D])]
    wgu_d = [din("ffn1_w_gu", [D, 2 * DFF]), din("ffn2_w_gu", [D, 2 * DFF])]
    wd_d = [din("ffn1_w_down", [DFF, D]), din("ffn2_w_down", [DFF, D])]
    win_d = din("w_in", [D, 1792])
    qn_d = din("q_norm", [1, 64])
    kn_d = din("k_norm", [1, 64])
    sinks_d = din("sinks", [1, 8])
    convw_d = din("conv_w", [CK, 512])
    convb_d = din("conv_b", [1, 512])
    lng_d = din("conv_ln_g", [1, 512])
    lnb_d = din("conv_ln_b", [1, 512])
    wout_d = din("w_out", [D, D])
    fin_d = din("final_norm", [1, D])
    ident_d = din("ident", [128, 128])
    jmat_d = din("jmat", [128, 128])
    ohr_d = din("ohr", [32, 512])

    yp = dout("yp", [NSEQ * SEQ, D])
    ys = dout("ys", [SB * DS, D])
    nkp = dout("nkp", [NSEQ, R, 128])
    nvp = dout("nvp", [NSEQ, R, 128])
    ncp = dout("ncp", [NSEQ, CK - 1, 512])
    nks = dout("nks", [SB, R, 128])
    nvs = dout("nvs", [SB, R, 128])
    ncs = dout("ncs", [SB, CK - 1, 512])

    s_gu = [nc.dram_tensor("s_gu%d" % f, [NM, 128, KD * 256], BF16).ap() for f in range(2)]
    s_d = [nc.dram_tensor("s_d%d" % f, [NM, 128, D], BF16).ap() for f in range(2)]
    s_in = nc.dram_tensor("s_in", [7, 128, KD * 256], BF16).ap()
    s_o = nc.dram_tensor("s_o", [KD, 128, D], BF16).ap()
    vr_d = nc.dram_tensor("vr_d", [8, 512], F32).ap()

    def sb(name, shape, dt=F32):
        return nc.alloc_sbuf_tensor("sb_" + name, list(shape), dt).ap()

    def ps(name, shape, dt=F32):
        return nc.alloc_psum_tensor("ps_" + name, list(shape), dt).ap()

    xs = [sb("xs%d" % i, [128, D]) for i in range(NXS)]
    xs_b = [Buf("xs%d" % i) for i in range(NXS)]
    gfin = sb("gfin", [128, D]); gfin_b = Buf("gfin")
    xq = [(sb("xnT%d" % i, [128, KD, GT], BF16), Buf("xnT%d" % i)) for i in range(2)]
    junk = sb("junk", [128, D], BF16); junk_b = Buf("junk")
    hT = sb("hT", [128, NM, GT], BF16); hT_bA = Buf("hTa"); hT_bB = Buf("hTb")
    MSPLIT = 18
    Dbuf = sb("Dbuf", [128, NM, D], BF16)
    D_b = [Buf("D%d" % m) for m in range(NM)]
    ring = Ring([(sb("ring%d" % i, [128, KD, 256], BF16), Buf("ring%d" % i), "ring%d" % i) for i in range(4)])
    xn_ring = Ring([(sb("xn%d" % i, [128, D], BF16), Buf("xn%d" % i)) for i in range(2)])
    tmpf = Ring([(sb("tf%d" % i, [128, 512]), Buf("tf%d" % i)) for i in range(6)])
    tmpb = Ring([(sb("tb%d" % i, [128, 512], BF16), Buf("tb%d" % i)) for i in range(4)])
    smr = Ring([(sb("sm%d" % i, [128, 16]), Buf("sm%d" % i)) for i in range(12)])
    qT = sb("qT", [128, 4, GT], BF16); qT_b = Buf("qT")
    kTe = sb("kTe", [128, 128 + GT], BF16); kTe_b = Buf("kTe")
    Vp = sb("Vp", [128, 5, 2, 128], BF16); Vp_b = Buf("Vp")
    uext = sb("uext", [128, 4, CK - 1 + GT], BF16); uext_b = Buf("uext")
    utail = sb("utail", [128, 4, 32]); utail_b = Buf("utail")
    dgr = Ring([(sb("dg%d" % i, [128, 128], BF16), Buf("dg%d" % i)) for i in range(12)])
    onec = sb("onec", [128, 1])
    acc = sb("acc", [128, 4, GT]); acc_b = Buf("acc"); acc_cb = [Buf("acc%d" % c) for c in range(4)]
    catT = sb("catT", [128, KD, GT], BF16); catA_b = Buf("catA"); catC_b = Buf("catC")
    ebias = [[sb("eb%d%d" % (g, kt), [128, 4, 128], BF16) for kt in range(2)] for g in range(2)]
    ebias_b = Buf("ebias")
    sinkexp = sb("sinkexp", [128, 4, 128]); cst_b = Buf("consts")
    identf = sb("identf", [128, 128]); identb = sb("identb", [128, 128], BF16)
    hTf = hT.rearrange("p m n -> p (m n)").bitcast(F32)
    jm = hTf[:, 0:128]; onesf = sb("onesf", [128, 128])
    blockones = sb("blockones", [128, 128], BF16)
    onespad = sb("onespad", [128, 2, 128], BF16)
    nhalf = sb("nhalf", [128, 8]); zeros = hTf[:, 1936:2064]
    prm = hTf[0:34, 1152:1664]; prm2 = hTf[0:24, 1664:1792]; prm3 = hTf[0:2, 1792:1920]
    prmT = sb("prmT", [128, 4, 34]); gcols = sb("gcols", [128, 24]); qkc = sb("qkc", [128, 2])
    sk = sb("sk", [128, 4]); ske = sb("ske", [128, 4]); epsc = sb("epsc", [128, 1])
    tbt = hTf[0:32, 1920:1928]; etab = hTf[0:32, 1928:1936]; ohr = hTf[0:32, 128:640]; vrs = hTf[0:8, 640:1152]
    kTc = sb("kTc", [128, SB, 128], BF16); Vpc = sb("Vpc", [128, SB, 2, 128], BF16)
    Vsn = sb("Vsn", [16, SB, 2, 128], BF16); uexs = sb("uexs", [128, 4, SB, CK - 1 + DS])
    smp_b = Buf("smp")

    pT_ring = Ring([(ps("pT%d" % i, [128, D], BF16), Buf("pT%d" % i)) for i in range(2)])
    P_items = [(ps("P%d" % i, [128, 512]), Buf("P%d" % i)) for i in range(6)]
    P_ring = Ring(P_items)
    PO = (P_items[0], P_items[1])
    PG_ring = Ring(P_items[2:5])
    PS_ring = Ring([(pT_ring.items[0][0].bitcast(F32), pT_ring.items[0][1]), (pT_ring.items[1][0].bitcast(F32), pT_ring.items[1][1]), P_items[5]])

    act = lambda fn, r=(), w=(): S.op('act', fn, r, w)
    dve = lambda fn, r=(), w=(): S.op('dve', fn, r, w)
    pool = lambda fn, r=(), w=(): S.op('pool', fn, r, w)
    pe = lambda fn, r=(), w=(): S.op('pe', fn, r, w)

    def mmgroup(out, pairs, first=True, last=True):
        def fn(e):
            ins = None
            n = len(pairs)
            for i, (l, r_) in enumerate(pairs):
                ins = e.matmul(out, lhsT=l, rhs=r_, start=(first and i == 0), stop=(last and i == n - 1))
            return ins
        return fn

    S.dma('sp', 'c0', [lambda e: e.dma_start(out=identf, in_=ident_d),
                       lambda e: e.dma_start(out=jm, in_=jmat_d),
                       lambda e: e.dma_start(out=ohr, in_=ohr_d),
                       lambda e: e.dma_start(out=tbt, in_=rbt),
                       lambda e: e.dma_start(out=gfin, in_=fin_d.to_broadcast([128, D])),
                       lambda e: e.dma_start(out=prm[0:CK, :], in_=convw_d),
                       lambda e: e.dma_start(out=prm[31:32, :], in_=convb_d),
                       lambda e: e.dma_start(out=prm[32:33, :], in_=lng_d),
                       lambda e: e.dma_start(out=prm[33:34, :], in_=lnb_d),
                       lambda e: e.dma_start(out=prm2[0:8, :], in_=gains_d[0].rearrange("o (k p) -> (o k) p", p=128)),
                       lambda e: e.dma_start(out=prm2[8:16, :], in_=gains_d[1].rearrange("o (k p) -> (o k) p", p=128)),
                       lambda e: e.dma_start(out=prm2[16:24, :], in_=gains_d[2].rearrange("o (k p) -> (o k) p", p=128)),
                       lambda e: e.dma_start(out=prm3[0:1, 0:64], in_=qn_d),
                       lambda e: e.dma_start(out=prm3[0:1, 64:128], in_=qn_d),
                       lambda e: e.dma_start(out=prm3[1:2, 0:64], in_=kn_d),
                       lambda e: e.dma_start(out=prm3[1:2, 64:128], in_=kn_d),
                       lambda e: e.dma_start(out=sk[0:64, :], in_=sinks_d[:, 0:4].to_broadcast([64, 4])),
                       lambda e: e.dma_start(out=sk[64:128, :], in_=sinks_d[:, 4:8].to_broadcast([64, 4])),
                       ], writes=[cst_b, gfin_b])
    dve(lambda e: e.tensor_copy(out=identb, in_=identf), [cst_b], [cst_b])
    dve(lambda e: e.memset(onesf, 1.0), [], [cst_b])
    dve(lambda e: e.memset(zeros, 0.0), [], [cst_b])
    dve(lambda e: e.memset(epsc, EPS), [], [cst_b])
    dve(lambda e: e.memset(onec, 1.0), [], [cst_b])
    dve(lambda e: e.memset(nhalf, -0.5), [], [cst_b])
    dve(lambda e: e.memset(blockones, 0.0), [], [cst_b])
    dve(lambda e: e.memset(onespad, 0.0), [], [cst_b])
    dve(lambda e: e.memset(Vp, 0.0), [], [Vp_b])
    dve(lambda e: e.memset(Vpc, 0.0), [], [smp_b])
    dve(lambda e: e.memset(Vsn, 0.0), [], [smp_b])
    dve(lambda e: e.memset(blockones[0:64, 0:64], 1.0), [cst_b], [cst_b])
    dve(lambda e: e.memset(blockones[64:128, 64:128], 1.0), [cst_b], [cst_b])
    dve(lambda e: e.memset(onespad[:, 0, 0:64], 1.0), [cst_b], [cst_b])
    dve(lambda e: e.memset(onespad[:, 1, 64:128], 1.0), [cst_b], [cst_b])
    pp, pp_b = P_ring.next()
    for c in range(4):
        pe(lambda e: e.transpose(out=pp[:, c * 34:(c + 1) * 34], in_=prm[0:34, c * 128:(c + 1) * 128],
                                 identity=identf[0:34, 0:34]), [cst_b], [pp_b])
    dve(lambda e: e.tensor_copy(out=prmT, in_=pp[:, 0:136].rearrange("p (c j) -> p c j", c=4)), [pp_b], [cst_b])
    pp, pp_b = P_ring.next()
    pe(lambda e: e.transpose(out=pp[:, 0:24], in_=prm2[0:24, :], identity=identf[0:24, 0:24]), [cst_b], [pp_b])
    pe(lambda e: e.transpose(out=pp[:, 32:34], in_=prm3[0:2, :], identity=identf[0:2, 0:2]), [cst_b], [pp_b])
    dve(lambda e: e.tensor_copy(out=gcols, in_=pp[:, 0:24]), [pp_b], [cst_b])
    dve(lambda e: e.tensor_scalar(out=qkc[:, 0:1], in0=pp[:, 32:33], scalar1=0.125, scalar2=None, op0=ALU.mult), [pp_b], [cst_b])
    dve(lambda e: e.tensor_copy(out=qkc[:, 1:2], in_=pp[:, 33:34]), [pp_b], [cst_b])
    act(lambda e: e.activation(out=ske, in_=sk, func=AF.Exp), [cst_b], [cst_b])
    for a in range(4):
        dve(lambda e: e.tensor_scalar(out=sinkexp[:, a, :], in0=zeros, scalar1=ske[:, a:a + 1], scalar2=None, op0=ALU.add),
            [cst_b], [cst_b])
    act(lambda e: e.copy(out=etab, in_=tbt), [cst_b], [cst_b])
    pp, pp_b = P_ring.next()
    pe(lambda e: e.matmul(pp[0:8, :], lhsT=etab, rhs=ohr, start=True, stop=True), [cst_b], [pp_b])
    dve(lambda e: e.tensor_copy(out=vrs, in_=pp[0:8, :]), [pp_b], [cst_b])
    vr_b = Buf("vr")
    S.dma('sp', 'vr', [lambda e: e.dma_start(out=vr_d, in_=vrs)], reads=[cst_b], writes=[vr_b])
    for g in range(2):
        for kt in range(2):
            hk, hk_b = tmpf.next()
            base = 256 if kt == 0 else 128
            src = bass.AP(vr_d.tensor, 4 * g * 512 + base, [[1, 128], [512, 4], [1, 128]])
            S.dma('sp', 'hk%d%d' % (g, kt), [lambda e: e.dma_start(out=hk.rearrange("p (a q) -> p a q", a=4), in_=src)],
                  reads=[vr_b], writes=[hk_b])
            pp, pp_b = P_ring.next()
            pe(lambda e: e.matmul(pp, lhsT=jm, rhs=hk, start=True, stop=True), [cst_b, hk_b], [pp_b])
            eb = ebias[g][kt]
            dve(lambda e: e.tensor_copy(out=eb, in_=pp.rearrange("p (a q) -> p a q", a=4)), [pp_b], [ebias_b])
            if kt == 0:
                dve(lambda e: e.memset(eb[0:64, :, 64:128], -30000.0), [ebias_b], [ebias_b])
            else:
                dve(lambda e: e.memset(eb[64:128, :, 0:64], -30000.0), [ebias_b], [ebias_b])

    Dflat = Dbuf.rearrange("p m n -> p (m n)")
    stg = [(Dflat[:, i * 5632:(i + 1) * 5632].bitcast(F32), Buf("stg%d" % i)) for i in range(3)]
    outb = [(Dflat[:, 16896 + i * 2816:16896 + (i + 1) * 2816], Buf("outb%d" % i)) for i in range(2)]
    job = [0]

    def prepass(srcs, conv, dsts, hook=None):
        i = job[0] % 2
        i3 = job[0] % 3
        job[0] += 1
        st, st_b = stg[i3]
        ob, ob_b = outb[i]
        S.dma('sp', 'ppin%d' % i3, [(lambda e, f=f: f(e, st)) for f in srcs], writes=[st_b])
        conv(st, st_b, ob, ob_b, i)
        S.dma('pool', 'ppout%d' % i, [(lambda e, f=f: f(e, ob)) for f in dsts], reads=[ob_b])
        if hook is not None:
            hook(ob, ob_b)

    def cvt(eng, o, i_, col, r, w):
        if eng == 'act':
            if col is None:
                act(lambda e: e.copy(out=o, in_=i_), r, w)
            else:
                act(lambda e: e.activation(out=o, in_=i_, func=AF.Copy, scale=col), r, w)
        else:
            if col is None:
                dve(lambda e: e.tensor_copy(out=o, in_=i_), r, w)
            else:
                dve(lambda e: e.tensor_scalar(out=o, in0=i_, scalar1=col, scalar2=None, op0=ALU.mult), r, w)

    def pp_gu(f, hook):
        gi = 0 if f == 0 else 2
        for k in range(KD):
            col = gcols[:, gi * 8 + k:gi * 8 + k + 1]
            rows = slice(k * 128, (k + 1) * 128)
            for mh in range(2):
                c0 = mh * 1408

                def conv(st, st_b, ob, ob_b, i, col=col):
                    o3 = ob.rearrange("p (m c) -> p m c", c=256)
                    cvt('act', o3[:, :, 0:128], st[:, 0:1408].rearrange("p (m c) -> p m c", c=128), col, [st_b, cst_b], [ob_b])
                    cvt('dve', o3[:, :, 128:256], st[:, 1408:2816].rearrange("p (m c) -> p m c", c=128), col, [st_b, cst_b], [ob_b])
                prepass([lambda e, st, c0=c0, rows=rows, f=f: e.dma_start(out=st[:, 0:1408], in_=wgu_d[f][rows, c0:c0 + 1408]),
                         lambda e, st, c0=c0, rows=rows, f=f: e.dma_start(out=st[:, 1408:2816], in_=wgu_d[f][rows, DFF + c0:DFF + c0 + 1408])],
                        conv,
                        [lambda e, ob, mh=mh, k=k, f=f: e.dma_start(
                            out=s_gu[f][mh * 11:(mh + 1) * 11, :, k * 256:(k + 1) * 256].rearrange("m p c -> p m c"),
                            in_=ob.rearrange("p (m c) -> p m c", c=256))],
                        (lambda ob, ob_b, k=k, mh=mh: hook(k, mh, ob, ob_b)) if hook else None)
    def pp_d(f, hook):
        for m0 in range(0, NM, 2):
            def conv(st, st_b, ob, ob_b, i, m0=m0):
                cvt('act' if (m0 // 2) % 2 == 0 else 'dve', ob[:, 0:2048], st[:, 0:2048], None, [st_b], [ob_b])
            prepass([lambda e, st, m0=m0, f=f: e.dma_start(out=st[:, 0:2048].rearrange("p (m n) -> p m n", m=2),
                                                         in_=wd_d[f][m0 * 128:(m0 + 2) * 128, :].rearrange("(m p) n -> p m n", p=128))],
                    conv,
                    [lambda e, ob, m0=m0, f=f: e.dma_start(out=s_d[f][m0:m0 + 2].rearrange("m p n -> p m n"),
                                                         in_=ob[:, 0:2048].rearrange("p (m n) -> p m n", m=2))],
                    (lambda ob, ob_b, m0=m0: hook(m0, ob, ob_b)) if hook else None)
    def pp_in(hook):
        for k in range(KD):
            col = gcols[:, 8 + k:8 + k + 1]

            def conv(st, st_b, ob, ob_b, i, col=col):
                cvt('act', ob[:, 0:512].rearrange("p (a g d) -> p a g d", a=4, g=2),
                    st[:, 0:512].rearrange("p (g a d) -> p a g d", g=2, a=4), col, [st_b, cst_b], [ob_b])
                cvt('dve', ob[:, 512:768], st[:, 512:768], col, [st_b, cst_b], [ob_b])
                o4 = ob[:, 768:1792].rearrange("p (c t n) -> p c t n", c=4, t=2)
                cvt('act', o4[:, :, 0, :], st[:, 768:1280].rearrange("p (c n) -> p c n", c=4), col, [st_b, cst_b], [ob_b])
                cvt('dve', o4[:, :, 1, :], st[:, 1280:1792].rearrange("p (c n) -> p c n", c=4), col, [st_b, cst_b], [ob_b])
            prepass([lambda e, st, k=k: e.dma_start(out=st[:, 0:1792], in_=win_d[k * 128:(k + 1) * 128, :])],
                    conv,
                    [lambda e, ob, k=k: e.dma_start(out=s_in[:, :, k * 256:(k + 1) * 256].rearrange("m p c -> p m c"),
                                                    in_=ob[:, 0:1792].rearrange("p (m c) -> p m c", c=256))],
                    (lambda ob, ob_b, k=k: hook(k, ob, ob_b)) if hook else None)
    def pp_o(hook):
        for j in range(4):
            kc0 = 2 * j
            srcs = []
            for i2 in range(2):
                kc = kc0 + i2
                if kc < 4:
                    srcs.append(lambda e, st, kc=kc, i2=i2: e.dma_start(out=st[0:64, i2 * 1024:(i2 + 1) * 1024], in_=wout_d[kc * 64:(kc + 1) * 64, :]))
                    srcs.append(lambda e, st, kc=kc, i2=i2: e.dma_start(out=st[64:128, i2 * 1024:(i2 + 1) * 1024], in_=wout_d[(4 + kc) * 64:(5 + kc) * 64, :]))
                else:
                    srcs.append(lambda e, st, kc=kc, i2=i2: e.dma_start(out=st[:, i2 * 1024:(i2 + 1) * 1024], in_=wout_d[512 + (kc - 4) * 128:512 + (kc - 3) * 128, :]))

            def conv(st, st_b, ob, ob_b, i, j=j):
                cvt('act' if j % 2 == 0 else 'dve', ob[:, 0:2048], st[:, 0:2048], None, [st_b], [ob_b])
            prepass(srcs, conv,
                    [lambda e, ob, kc0=kc0: e.dma_start(out=s_o[kc0:kc0 + 2].rearrange("m p n -> p m n"),
                                                        in_=ob[:, 0:2048].rearrange("p (m n) -> p m n", m=2))],
                    (lambda ob, ob_b, j=j: hook(j, ob, ob_b)) if hook else None)

    groups = []
    for b in range(NSEQ):
        for gi in range(NGRP):
            groups.append(dict(kind='p', seq=b, gi=gi, row0=b * SEQ + gi * GT, NT=4, TP=128, N=GT))
    GS = dict(kind='s', seq=0, gi=0, row0=0, NT=1, TP=64, N=64)
    tid = 0
    for G in groups:
        G['slots'] = [(tid + t) % NXS for t in range(G['NT'])]
        tid += G['NT']
    GS['slots'] = [NXS - 1]

    def x_load(G, tiles):
        for t in tiles:
            sl = G['slots'][t]
            TP = G['TP']
            src = (xp if G['kind'] == 'p' else xsm)[G['row0'] + t * TP:G['row0'] + (t + 1) * TP, :]
            S.dma('pool', 'xl%d' % sl, [lambda e: e.dma_start(out=xs[sl][0:TP, :], in_=src)], writes=[xs_b[sl]])

    def x_store(G, t):
        sl = G['slots'][t]
        TP = G['TP']
        dst = (yp if G['kind'] == 'p' else ys)[G['row0'] + t * TP:G['row0'] + (t + 1) * TP, :]
        S.dma('pool', 'xst%d' % sl, [lambda e: e.dma_start(out=dst, in_=xs[sl][0:TP, :])], reads=[xs_b[sl]])

    blocks = []
    for gidx in range(len(groups)):
        for m in range(NM):
            blocks.append(('gu', 0, m))
        for j in range(7):
            blocks.append(('in', 0, j))
        for m in range(NM):
            blocks.append(('gu', 1, m))
    nfill = [0]
    slot_of = {}

    def fill(i):
        kind, f, m = blocks[i]
        rap, rb, rname = ring.next()
        slot_of[i] = (rap, rb)
        src = s_gu[f][m] if kind == 'gu' else s_in[m]
        S.dma('sp', rname, [lambda e: e.dma_start(out=rap.rearrange("p k c -> p (k c)"), in_=src)], writes=[rb])

    def d_load(chunks, src):
        for dm in chunks:
            S.dma('sp', 'D%d' % dm, [lambda e: e.dma_start(out=Dbuf[:, dm, :], in_=src[dm])], writes=[D_b[dm]])

    def get_block(i):
        while nfill[0] <= i + 3 and nfill[0] < len(blocks):
            fill(nfill[0])
            nfill[0] += 1
        return slot_of.pop(i)

    blk = [0]

    def next_block():
        r = get_block(blk[0])
        blk[0] += 1
        return r

    ph = [0]

    def cur_x():
        return xq[ph[0] % 2]

    def nxt_x():
        return xq[(ph[0] + 1) % 2]

    def rms_chain(G, t):
        TP = G['TP']
        sl = G['slots'][t]
        ssq, ssq_b = smr.next()
        act(lambda e: e.activation(out=junk[0:TP, :], in_=xs[sl][0:TP, :], func=AF.Square, accum_out=ssq[0:TP, 0:1]),
            [xs_b[sl], junk_b], [junk_b, ssq_b])
        dve(lambda e: e.tensor_scalar(out=ssq[0:TP, 1:2], in0=ssq[0:TP, 0:1], scalar1=1.0 / D, scalar2=EPS, op0=ALU.mult, op1=ALU.add),
            [ssq_b], [ssq_b])
        pool(lambda e: e.tensor_tensor(out=ssq[0:TP, 2:3], in0=ssq[0:TP, 1:2], in1=nhalf[0:TP, 0:1], op=ALU.pow), [ssq_b, cst_b], [ssq_b])
        xn, xn_b = xn_ring.next()
        dve(lambda e: e.tensor_scalar(out=xn[0:TP, :], in0=xs[sl][0:TP, :], scalar1=ssq[0:TP, 2:3], scalar2=None, op0=ALU.mult),
            [xs_b[sl], ssq_b], [xn_b])
        return (t, xn, xn_b)

    def rms_tr(G, item, dst):
        TP = G['TP']
        t, xn, xn_b = item
        xT, xT_b = dst
        pT, pT_b = pT_ring.next()

        def tr(e):
            ins = None
            for k in range(KD):
                ins = e.transpose(out=pT[:, k * 128:k * 128 + TP], in_=xn[0:TP, k * 128:(k + 1) * 128], identity=identb[0:TP, 0:TP])
            return ins
        pe(tr, [xn_b, cst_b], [pT_b])
        act(lambda e: e.copy(out=xT[:, :, t * TP:(t + 1) * TP], in_=pT.rearrange("p (k n) -> p k n", k=KD)[:, :, 0:TP]),
            [pT_b], [xT_b])

    def ffn(G, f, Gn):
        NT, TP, N = G['NT'], G['TP'], G['N']
        xnT, xnT_b = cur_x()
        gu_pend = []
        for m in range(NM):
            w, w_b = next_block()
            d_load(([0, 1, 2] if m == 0 else ([m + 2] if m + 2 < NM else [])), s_d[f])
            pG, pG_b = P_ring.next()
            pU, pU_b = P_ring.next()
            pe(mmgroup(pG[:, 0:N], [(w[:, k, 0:128], xnT[:, k, 0:N]) for k in range(KD)]), [w_b, xnT_b], [pG_b])
            pe(mmgroup(pU[:, 0:N], [(w[:, k, 128:256], xnT[:, k, 0:N]) for k in range(KD)]), [w_b, xnT_b], [pU_b])
            sg, sg_b = tmpf.next()
            act(lambda e: e.activation(out=sg[:, 0:N], in_=pG[:, 0:N], func=AF.Silu), [pG_b], [sg_b])
            dve(lambda e: e.tensor_tensor(out=hT[:, m, 0:N], in0=sg[:, 0:N], in1=pU[:, 0:N], op=ALU.mult), [sg_b, pU_b],
                [hT_bA if m < MSPLIT else hT_bB])
            if f == 1 and Gn is not None:
                for tt in range(Gn['NT']):
                    if m == 4 + 4 * tt:
                        gu_pend.append(rms_chain(Gn, tt))
                    if m == 6 + 4 * tt:
                        rms_tr(Gn, gu_pend.pop(0), nxt_x())
        pend = []
        pendn = []
        done_n = []
        for t in range(NT):
            sl = G['slots'][t]
            for hf in range(2):
                pO, pO_b = P_ring.next()
                pe(mmgroup(pO[0:TP, :], [(hT[:, m, t * TP:(t + 1) * TP], Dbuf[:, m, hf * 512:(hf + 1) * 512]) for m in range(MSPLIT)],
                           first=True, last=False), [hT_bA] + D_b[0:MSPLIT], [pO_b])
                pe(mmgroup(pO[0:TP, :], [(hT[:, m, t * TP:(t + 1) * TP], Dbuf[:, m, hf * 512:(hf + 1) * 512]) for m in range(MSPLIT, NM)],
                           first=False, last=True), [hT_bB] + D_b[MSPLIT:NM], [pO_b])
                xsl = xs[sl][0:TP, hf * 512:(hf + 1) * 512]
                dve(lambda e: e.scalar_tensor_tensor(out=xsl, in0=pO[0:TP, :], scalar=0.5, in1=xsl, op0=ALU.mult, op1=ALU.add),
                    [pO_b, xs_b[sl]], [xs_b[sl]])
            if f == 0:
                pend.append(rms_chain(G, t))
                if len(pend) > 1:
                    rms_tr(G, pend.pop(0), nxt_x())
            else:
                final_norm_store(G, t)
        for it in pend:
            rms_tr(G, it, nxt_x())
        ph[0] += 1

    def final_norm_store(G, t):
        TP = G['TP']
        sl = G['slots'][t]
        ssq, ssq_b = smr.next()
        act(lambda e: e.activation(out=junk[0:TP, :], in_=xs[sl][0:TP, :], func=AF.Square, accum_out=ssq[0:TP, 0:1]),
            [xs_b[sl], junk_b], [junk_b, ssq_b])
        dve(lambda e: e.tensor_scalar(out=ssq[0:TP, 1:2], in0=ssq[0:TP, 0:1], scalar1=1.0 / D, scalar2=EPS, op0=ALU.mult, op1=ALU.add),
            [ssq_b], [ssq_b])
        pool(lambda e: e.tensor_tensor(out=ssq[0:TP, 2:3], in0=ssq[0:TP, 1:2], in1=nhalf[0:TP, 0:1], op=ALU.pow), [ssq_b, cst_b], [ssq_b])
        dve(lambda e: e.scalar_tensor_tensor(out=xs[sl][0:TP, :], in0=xs[sl][0:TP, :], scalar=ssq[0:TP, 2:3], in1=gfin[0:TP, :],
                                             op0=ALU.mult, op1=ALU.mult), [xs_b[sl], ssq_b, gfin_b], [xs_b[sl]])
        x_store(G, t)

    def headnorm_a(pQ, pQ_b, N):
        sq, sq_b = tmpb.next()
        act(lambda e: e.activation(out=sq[:, 0:N], in_=pQ[:, 0:N], func=AF.Square), [pQ_b], [sq_b])
        pS, pS_b = P_ring.next()
        pe(lambda e: e.matmul(pS[:, 0:N], lhsT=blockones, rhs=sq[:, 0:N], start=True, stop=True), [sq_b, cst_b], [pS_b])
        return pS, pS_b

    def headnorm_b(pQ, pQ_b, pS, pS_b, N, gcol, outs):
        r, r_b = tmpf.next()
        act(lambda e: e.activation(out=r[:, 0:N], in_=pS[:, 0:N], func=AF.Ln, scale=1.0 / 64, bias=epsc[:, 0:1]), [pS_b, cst_b], [r_b])
        act(lambda e: e.activation(out=r[:, 0:N], in_=r[:, 0:N], func=AF.Exp, scale=-0.5), [r_b], [r_b])
        for (o, cs, ob) in outs:
            dve(lambda e: e.scalar_tensor_tensor(out=o, in0=pQ[:, cs], scalar=gcol, in1=r[:, cs], op0=ALU.mult, op1=ALU.mult),
                [pQ_b, r_b, cst_b], [ob])

    def mixer(G, pre=None):
        NT, TP, N = G['NT'], G['TP'], G['N']
        prompt = G['kind'] == 'p'
        last = prompt and G['gi'] == NGRP - 1
        first = prompt and G['gi'] == 0
        if pre is None:
            xnT, xnT_b = cur_x()
        kf = kf_b = None
        tiles_qk = []
        for qb in range(2):
            for j in range(2):
                tiles_qk.append(('q', qb, j))
        tiles_qk.append(('k', 2, 0))
        blkw = {}

        def proj_qk(i):
            if pre is not None:
                return pre['qk'][i]
            if i < 4:
                d_load([2 * i, 2 * i + 1], s_o)
            kind, bi, j = tiles_qk[i]
            if bi not in blkw:
                blkw[bi] = next_block()
            w, w_b = blkw[bi]
            pQ, pQ_b = P_ring.next()
            pe(mmgroup(pQ[:, 0:N], [(w[:, k, j * 128:(j + 1) * 128], xnT[:, k, 0:N]) for k in range(KD)]), [w_b, xnT_b], [pQ_b])
            return pQ, pQ_b

        diag = {}

        def build_diags(lo, hi):
            npt = CK - NDVE
            for idx in range(lo, hi):
                c, j = idx // npt, NDVE + idx % npt
                if idx < 112:
                    ch = 8 + idx // 8
                    dg = Dbuf[:, ch, (idx % 8) * 128:(idx % 8 + 1) * 128]
                    dg_b = D_b[ch]
                else:
                    dg, dg_b = dgr.next()
                pool(lambda e: e.tensor_tensor(out=dg, in0=identb, in1=prmT[:, c, j:j + 1].to_broadcast([128, 128]), op=ALU.mult), [cst_b], [dg_b])
                diag[(c, j)] = (dg, dg_b)

        if prompt:
            build_diags(0, 4 * (CK - NDVE))
        def stage_b(i, pq, pss):
            pQ, pQ_b = pq
            pS, pS_b = pss
            nonlocal_kf = None
            if tiles_qk[i][0] == 'q':
                a = 2 * tiles_qk[i][1] + tiles_qk[i][2]
                headnorm_b(pQ, pQ_b, pS, pS_b, N, qkc[:, 0:1], [(qT[:, a, 0:N], slice(0, N), qT_b)])
            else:
                outs = [(kTe[:, 128:128 + N], slice(0, N), kTe_b)]
                if last or not prompt:
                    kfl = tmpf.next()
                    nk = 128 if prompt else N
                    outs.append((kfl[0][:, 0:nk], slice(N - nk, N), kfl[1]))
                    nonlocal_kf = kfl
                headnorm_b(pQ, pQ_b, pS, pS_b, N, qkc[:, 1:2], outs)
            return nonlocal_kf

        pqs = {}
        pss = {}
        pqs[0] = proj_qk(0)
        pqs[1] = proj_qk(1)
        pss[0] = headnorm_a(pqs[0][0], pqs[0][1], N)
        for i in range(5):
            if i + 2 < 5:
                pqs[i + 2] = proj_qk(i + 2)
            if i + 1 < 5:
                pss[i + 1] = headnorm_a(pqs[i + 1][0], pqs[i + 1][1], N)
            r_ = stage_b(i, pqs[i], pss[i])
            if r_ is not None:
                kf, kf_b = r_
        if pre is None:
            w, w_b = blkw[2]
        if kf is not None:
            nk = 128 if prompt else N
            pX, pX_b = PG_ring.next()
            pe(lambda e: e.transpose(out=pX[0:nk, 0:128], in_=kf[:, 0:nk], identity=identf), [kf_b, cst_b], [pX_b])
            ko, ko_b = tmpf.next()
            dve(lambda e: e.tensor_copy(out=ko[0:nk, 0:128], in_=pX[0:nk, 0:128]), [pX_b], [ko_b])
            if prompt:
                S.dma('pool', 'oko', [lambda e: e.dma_start(out=nkp[G['seq']], in_=ko[:, 0:128])], reads=[ko_b])
            else:
                for s_ in range(SB):
                    S.dma('pool', 'oko', [lambda e: e.dma_start(out=nks[s_, R - DS:R, :], in_=ko[s_ * DS:(s_ + 1) * DS, 0:128])], reads=[ko_b])
        if prompt:
            for t in range(NT):
                pV, pV_b = PG_ring.next()
                pe(mmgroup(pV[0:TP, 0:128], [(xnT[:, k, t * TP:(t + 1) * TP], w[:, k, 128:256]) for k in range(KD)]), [w_b, xnT_b], [pV_b])
                act(lambda e: e.copy(out=Vp[:, 1 + t, 0, 0:64], in_=pV[:, 0:64]), [pV_b], [Vp_b])
                dve(lambda e: e.tensor_copy(out=Vp[:, 1 + t, 1, 64:128], in_=pV[:, 64:128]), [pV_b], [Vp_b])
                if last and t == NT - 1:
                    vo, vo_b = tmpf.next()
                    dve(lambda e: e.tensor_copy(out=vo[:, 0:128], in_=pV[:, 0:128]), [pV_b], [vo_b])
                    S.dma('pool', 'ovo', [lambda e: e.dma_start(out=nvp[G['seq']], in_=vo[:, 0:128])], reads=[vo_b])
        elif pre is None:
            for s_ in range(SB):
                pV, pV_b = PG_ring.next()
                pe(mmgroup(pV[0:DS, 0:128], [(xnT[:, k, s_ * DS:(s_ + 1) * DS], w[:, k, 128:256]) for k in range(KD)]), [w_b, xnT_b], [pV_b])
                act(lambda e: e.copy(out=Vsn[:, s_, 0, 0:64], in_=pV[0:DS, 0:64]), [pV_b], [smp_b])
                dve(lambda e: e.tensor_copy(out=Vsn[:, s_, 1, 64:128], in_=pV[0:DS, 64:128]), [pV_b], [smp_b])
                vo, vo_b = tmpf.next()
                dve(lambda e: e.tensor_copy(out=vo[0:DS, 0:128], in_=pV[0:DS, 0:128]), [pV_b], [vo_b])
                S.dma('pool', 'ovo%d' % s_, [lambda e: e.dma_start(out=nvs[s_, R - DS:R, :], in_=vo[0:DS, 0:128])], reads=[vo_b])
        if first:
            dve(lambda e: e.memset(uext[:, :, 0:CK - 1], 0.0), [], [uext_b])
        ub = uext_b if prompt else smp_b

        def usl(c, j):
            return uexs[:, c, :, j:j + DS]

        def asl(c):
            return acc[:, c, 0:N].rearrange("p (s i) -> p s i", s=SB)

        if prompt:
            units = [dict(q0=p * 128, nq=128, tiles=[('A', kTe[:, p * 128:(p + 1) * 128], Vp[:, p], 0, 128),
                                                    ('B', kTe[:, (p + 1) * 128:(p + 2) * 128], Vp[:, p + 1], 1, 128)][(1 if (first and p == 0) else 0):])
                     for p in range(4)]
        else:
            units = [dict(q0=s_ * DS, nq=DS, tiles=[('A', kTc[:, s_, :], Vpc[:, s_], 0, 128),
                                                    ('B', kTe[:, 128 + s_ * DS:128 + (s_ + 1) * DS], Vsn[:, s_], 1, DS)])
                     for s_ in range(SB)]
        kvb = [kTe_b, Vp_b, smp_b, qT_b, ebias_b, cst_b]
        (pO1, pO1_b), (pO2, pO2_b) = PO

        def att_part1(U):
            q0, nq = U['q0'], U['nq']
            NQ = 4 * nq
            U['steps'] = []
            for g in range(2):
                for (nm, kap, vap, kt, nkeys) in U['tiles']:
                    gs = slice(64 * g, 64 * g + 64)
                    pSc, pSc_b = PS_ring.next()
                    sc_o = pSc[0:nkeys, 0:NQ].rearrange("p (a n) -> p a n", a=4)
                    def scfn(e):
                        e.matmul(sc_o, lhsT=kap[gs, :], rhs=qT[gs, :, q0:q0 + nq], start=True, stop=False)
                        return e.matmul(sc_o, lhsT=identb[0:nkeys, 0:nkeys], rhs=ebias[g][kt][0:nkeys, :, 0:nq], start=False, stop=True)
                    pe(scfn, kvb, [pSc_b])
                    et, et_b = tmpb.next()
                    act(lambda e: e.activation(out=et[0:nkeys, 0:NQ], in_=pSc[0:nkeys, 0:NQ], func=AF.Exp), [pSc_b], [et_b])
                    U['steps'].append((g, vap, nkeys, et, et_b))

        def att_part2(U):
            q0, nq = U['q0'], U['nq']
            NQ = 4 * nq
            if prompt:
                o1 = pO1[:, 0:NQ].rearrange("p (a n) -> p a n", a=4)
                o2 = pO2[:, 0:NQ].rearrange("p (a n) -> p a n", a=4)
            else:
                o1 = pO1[:, 0:256].rearrange("p (a n) -> p a n", a=4)[:, :, q0:q0 + nq]
                o2 = pO2[:, 0:256].rearrange("p (a n) -> p a n", a=4)[:, :, q0:q0 + nq]
            ns = len(U['steps'])
            for si, (g, vap, nkeys, et, et_b) in enumerate(U['steps']):
                etv = et[0:nkeys, 0:NQ].rearrange("p (a n) -> p a n", a=4)
                st_, sp_ = (si == 0), (si == ns - 1)
                def pvfn(e):
                    e.matmul(o1, lhsT=vap[0:nkeys, g, :], rhs=etv, start=st_, stop=sp_)
                    return e.matmul(o2, lhsT=onespad[0:nkeys, g, :], rhs=etv, start=st_, stop=sp_)
                pe(pvfn, [et_b] + kvb, [pO1_b, pO2_b])
            if prompt:
                den, den_b = tmpf.next()
                dve(lambda e: e.tensor_tensor(out=den, in0=pO2, in1=sinkexp.rearrange("p a n -> p (a n)"), op=ALU.add), [pO2_b, cst_b], [den_b])
                act(lambda e: e.activation(out=den, in_=den, func=AF.Ln), [den_b], [den_b])
                act(lambda e: e.activation(out=den, in_=den, func=AF.Exp, scale=-1.0), [den_b], [den_b])
                dve(lambda e: e.tensor_tensor(out=catT[:, 0:4, q0:q0 + 128], in0=pO1.rearrange("p (a n) -> p a n", a=4),
                                              in1=den.rearrange("p (a n) -> p a n", a=4), op=ALU.mult), [pO1_b, den_b], [catA_b])

        def glu(c):
            if pre is not None:
                pA, pA_b = pre['A'][c]
                pGg, pGg_b = pre['G'][c]
            else:
                w, w_b = next_block()
                pA, pA_b = PG_ring.next()
                pGg, pGg_b = PG_ring.next()
                pe(mmgroup(pA[:, 0:N], [(w[:, k, 0:128], xnT[:, k, 0:N]) for k in range(KD)]), [w_b, xnT_b], [pA_b])
                pe(mmgroup(pGg[:, 0:N], [(w[:, k, 128:256], xnT[:, k, 0:N]) for k in range(KD)]), [w_b, xnT_b], [pGg_b])
            th, th_b = tmpf.next()
            act(lambda e: e.activation(out=th[:, 0:N], in_=pGg[:, 0:N], func=AF.Exp, scale=-1.0), [pGg_b], [th_b])
            act(lambda e: e.activation(out=th[:, 0:N], in_=th[:, 0:N], func=AF.Ln, bias=onec[:, 0:1]), [th_b, cst_b], [th_b])
            act(lambda e: e.activation(out=th[:, 0:N], in_=th[:, 0:N], func=AF.Exp, scale=-1.0), [th_b], [th_b])
            if prompt:
                dve(lambda e: e.tensor_tensor(out=uext[:, c, CK - 1:CK - 1 + N], in0=pA[:, 0:N], in1=th[:, 0:N], op=ALU.mult),
                    [pA_b, th_b], [uext_b])
                if last:
                    dve(lambda e: e.tensor_tensor(out=utail[:, c, 0:CK - 1], in0=pA[:, N - (CK - 1):N], in1=th[:, N - (CK - 1):N], op=ALU.mult),
                        [pA_b, th_b], [utail_b])
                dve(lambda e: e.tensor_scalar(out=acc[:, c, 0:N], in0=uext[:, c, 0:N], scalar1=prmT[:, c, 0:1], scalar2=prmT[:, c, 31:32],
                                              op0=ALU.mult, op1=ALU.add), [uext_b, cst_b], [acc_cb[c]])
                for j in range(1, NDVE):
                    dve(lambda e: e.scalar_tensor_tensor(out=acc[:, c, 0:N], in0=uext[:, c, j:j + N], scalar=prmT[:, c, j:j + 1], in1=acc[:, c, 0:N],
                                                         op0=ALU.mult, op1=ALU.add), [uext_b, cst_b, acc_cb[c]], [acc_cb[c]])
            else:
                dve(lambda e: e.tensor_tensor(out=uexs[:, c, :, CK - 1:CK - 1 + DS], in0=pA[:, 0:N].rearrange("p (s i) -> p s i", s=SB),
                                              in1=th[:, 0:N].rearrange("p (s i) -> p s i", s=SB), op=ALU.mult), [pA_b, th_b], [smp_b])

        glu(0)
        for c in range(4):
            if c + 1 < 4:
                glu(c + 1)
            att_part1(units[c])
            if prompt:
                pc, pc_b = PG_ring.next()
                dbs = []
                for j in range(NDVE, CK):
                    if diag[(c, j)][1] not in dbs:
                        dbs.append(diag[(c, j)][1])
                pe(mmgroup(pc[:, 0:N], [(diag[(c, j)][0], uext[:, c, j:j + N]) for j in range(NDVE, CK)]), dbs + [uext_b], [pc_b])
                dve(lambda e: e.tensor_tensor(out=acc[:, c, 0:N], in0=pc[:, 0:N], in1=acc[:, c, 0:N], op=ALU.add),
                    [pc_b, acc_cb[c]], [acc_cb[c]])
            elif c == 3:
                for cc in range(4):
                    dve(lambda e: e.tensor_scalar(out=asl(cc), in0=usl(cc, 0), scalar1=prmT[:, cc, 0:1], scalar2=prmT[:, cc, 31:32],
                                                  op0=ALU.mult, op1=ALU.add), [ub, cst_b, acc_b], [acc_cb[cc]])
                for j in range(1, CK):
                    for cc in range(4):
                        dve(lambda e: e.scalar_tensor_tensor(out=asl(cc), in0=usl(cc, j), scalar=prmT[:, cc, j:j + 1], in1=asl(cc),
                                                             op0=ALU.mult, op1=ALU.add), [ub, cst_b, acc_cb[cc]], [acc_cb[cc]])
            att_part2(units[c])
        if not prompt:
            den, den_b = tmpf.next()
            d3 = den[:, 0:256].rearrange("p (a n) -> p a n", a=4)
            dve(lambda e: e.tensor_tensor(out=d3, in0=pO2[:, 0:256].rearrange("p (a n) -> p a n", a=4), in1=sinkexp[:, :, 0:64], op=ALU.add),
                [pO2_b, cst_b], [den_b])
            act(lambda e: e.activation(out=den[:, 0:256], in_=den[:, 0:256], func=AF.Ln), [den_b], [den_b])
            act(lambda e: e.activation(out=den[:, 0:256], in_=den[:, 0:256], func=AF.Exp, scale=-1.0), [den_b], [den_b])
            dve(lambda e: e.tensor_tensor(out=catT[:, 0:4, 0:64], in0=pO1[:, 0:256].rearrange("p (a n) -> p a n", a=4), in1=d3, op=ALU.mult),
                [pO1_b, den_b], [catA_b])
        if prompt and last:
            pX, pX_b = PG_ring.next()
            for c in range(4):
                pe(lambda e: e.transpose(out=pX[0:CK - 1, c * 128:(c + 1) * 128], in_=utail[:, c, 0:CK - 1], identity=identf),
                   [utail_b, cst_b], [pX_b])
            uo, uo_b = tmpf.next()
            dve(lambda e: e.tensor_copy(out=uo[0:CK - 1, :], in_=pX[0:CK - 1, :]), [pX_b], [uo_b])
            S.dma('pool', 'ouo', [lambda e: e.dma_start(out=ncp[G['seq']], in_=uo[0:CK - 1, :])], reads=[uo_b])
        elif prompt:
            act(lambda e: e.copy(out=uext[:, :, 0:CK - 1], in_=uext[:, :, GT:GT + CK - 1]), [uext_b], [uext_b])
        else:
            for s_ in range(SB):
                pX, pX_b = PG_ring.next()
                for c in range(4):
                    pe(lambda e: e.transpose(out=pX[0:DS, c * 128:(c + 1) * 128], in_=uexs[:, c, s_, CK - 1:CK - 1 + DS], identity=identf),
                       [smp_b, cst_b], [pX_b])
                uo, uo_b = tmpf.next()
                dve(lambda e: e.tensor_copy(out=uo[0:DS, :], in_=pX[0:DS, :]), [pX_b], [uo_b])
                S.dma('pool', 'ouo%d' % s_, [lambda e: e.dma_start(out=ncs[s_, CK - 1 - DS:CK - 1, :], in_=uo[0:DS, :])], reads=[uo_b])
        if prompt and not last:
            act(lambda e: e.copy(out=kTe[:, 0:128], in_=kTe[:, GT:GT + 128]), [kTe_b], [kTe_b])
            dve(lambda e: e.tensor_copy(out=Vp[:, 0], in_=Vp[:, 4]), [Vp_b], [Vp_b])
        pM, pM_b = PG_ring.next()
        pV2, pV2_b = PG_ring.next()
        pe(mmgroup(pM[:, 0:N], [(onesf, acc[:, c, 0:N]) for c in range(4)]), [acc_b, cst_b] + acc_cb, [pM_b])
        ysqs = []
        for c in range(4):
            ysq, ysq_b = tmpf.next()
            act(lambda e: e.activation(out=ysq[:, 0:N], in_=acc[:, c, 0:N], func=AF.Square), [acc_b, acc_cb[c]], [ysq_b])
            ysqs.append((ysq, ysq_b))
        pe(mmgroup(pV2[:, 0:N], [(onesf, y_[:, 0:N]) for (y_, _) in ysqs]), [b_ for (_, b_) in ysqs] + [cst_b], [pV2_b])
        pgx = PG_ring.next()
        obanks = [(PS_ring.items[0], PS_ring.items[1]), (PO[0], PO[1]), (PS_ring.items[2], pgx)]
        n_early = min(NT, 3) if pre is None else 0
        for t in range(n_early):
            for hf in range(2):
                pO, pO_b = obanks[t][hf]
                pe(mmgroup(pO[0:TP, :], [(catT[:, kc, t * TP:(t + 1) * TP], Dbuf[:, kc, hf * 512:(hf + 1) * 512]) for kc in range(4)],
                           first=True, last=False), [catA_b] + D_b[0:4], [pO_b])
        mean, mean_b = tmpf.next()
        dve(lambda e: e.tensor_scalar(out=mean[:, 0:N], in0=pM[:, 0:N], scalar1=1.0 / 512, scalar2=None, op0=ALU.mult), [pM_b], [mean_b])
        var, var_b = tmpf.next()
        dve(lambda e: e.tensor_tensor(out=var[:, 0:N], in0=mean[:, 0:N], in1=mean[:, 0:N], op=ALU.mult), [mean_b], [var_b])
        dve(lambda e: e.scalar_tensor_tensor(out=var[:, 0:N], in0=pV2[:, 0:N], scalar=1.0 / 512, in1=var[:, 0:N], op0=ALU.mult, op1=ALU.subtract),
            [pV2_b, var_b], [var_b])
        rstd, rstd_b = tmpf.next()
        act(lambda e: e.activation(out=rstd[:, 0:N], in_=var[:, 0:N], func=AF.Ln, bias=epsc[:, 0:1]), [var_b, cst_b], [rstd_b])
        act(lambda e: e.activation(out=rstd[:, 0:N], in_=rstd[:, 0:N], func=AF.Exp, scale=-0.5), [rstd_b], [rstd_b])
        for c in range(4):
            t1, t1_b = tmpf.next()
            dve(lambda e: e.tensor_tensor(out=t1[:, 0:N], in0=acc[:, c, 0:N], in1=mean[:, 0:N], op=ALU.subtract), [acc_b, acc_cb[c], mean_b], [t1_b])
            dve(lambda e: e.tensor_tensor(out=t1[:, 0:N], in0=t1[:, 0:N], in1=rstd[:, 0:N], op=ALU.mult), [t1_b, rstd_b], [t1_b])
            act(lambda e: e.activation(out=catT[:, 4 + c, 0:N], in_=t1[:, 0:N], func=AF.Silu, scale=prmT[:, c, 32:33], bias=prmT[:, c, 33:34]),
                [t1_b, cst_b], [catC_b])
        if pre is not None:
            return
        pend = []
        for t in range(NT):
            sl = G['slots'][t]
            for hf in range(2):
                if t < n_early:
                    pO, pO_b = obanks[t][hf]
                    pe(mmgroup(pO[0:TP, :], [(catT[:, kc, t * TP:(t + 1) * TP], Dbuf[:, kc, hf * 512:(hf + 1) * 512]) for kc in range(4, KD)],
                               first=False, last=True), [catC_b] + D_b[4:KD], [pO_b])
                else:
                    pO, pO_b = PG_ring.next()
                    pe(mmgroup(pO[0:TP, :], [(catT[:, kc, t * TP:(t + 1) * TP], Dbuf[:, kc, hf * 512:(hf + 1) * 512]) for kc in range(KD)]),
                       [catA_b, catC_b] + D_b[0:KD], [pO_b])
                xsl = xs[sl][0:TP, hf * 512:(hf + 1) * 512]
                dve(lambda e: e.tensor_tensor(out=xsl, in0=pO[0:TP, :], in1=xsl, op=ALU.add), [pO_b, xs_b[sl]], [xs_b[sl]])
            pend.append(rms_chain(G, t))
            if len(pend) > 1:
                rms_tr(G, pend.pop(0), nxt_x())
        for it in pend:
            rms_tr(G, it, nxt_x())
        ph[0] += 1

    def sample_prep():
        st, st_b = tmpf.next()
        S.dma('pool', 'smp0', [lambda e: e.dma_start(out=st.rearrange("p (s c) -> p s c", s=SB), in_=ck.rearrange("s r c -> r s c"))], writes=[st_b])
        cb, cb_b = tmpb.next()
        dve(lambda e: e.tensor_copy(out=cb, in_=st), [st_b], [cb_b])
        pT, pT_b = pT_ring.next()

        def tr(e):
            ins = None
            for s_ in range(SB):
                ins = e.transpose(out=pT[:, s_ * 128:(s_ + 1) * 128], in_=cb[:, s_ * 128:(s_ + 1) * 128], identity=identb)
            return ins
        pe(tr, [cb_b, cst_b], [pT_b])
        act(lambda e: e.copy(out=kTc, in_=pT[:, 0:512].rearrange("p (s n) -> p s n", s=SB)), [pT_b], [smp_b])
        st2, st2_b = tmpf.next()
        S.dma('pool', 'smp1', [lambda e: e.dma_start(out=st2.rearrange("p (s c) -> p s c", s=SB), in_=cv.rearrange("s r c -> r s c"))], writes=[st2_b])
        s3 = st2.rearrange("p (s c) -> p s c", s=SB)
        dve(lambda e: e.tensor_copy(out=Vpc[:, :, 0, 0:64], in_=s3[:, :, 0:64]), [st2_b], [smp_b])
        dve(lambda e: e.tensor_copy(out=Vpc[:, :, 1, 64:128], in_=s3[:, :, 64:128]), [st2_b], [smp_b])
        stc = acc[0:CK - 1, :, :]
        S.dma('pool', 'smp2', [lambda e: e.dma_start(out=stc, in_=sc.rearrange("s r c -> r s c"))], writes=[acc_b])
        pX, pX_b = P_ring.next()
        for c in range(4):
            for s_ in range(SB):
                o = pX[:, (c * SB + s_) * (CK - 1):(c * SB + s_ + 1) * (CK - 1)]
                pe(lambda e: e.transpose(out=o, in_=stc[:, s_, c * 128:(c + 1) * 128], identity=identf[0:CK - 1, 0:CK - 1]), [acc_b, cst_b], [pX_b])
        dve(lambda e: e.tensor_copy(out=uexs[:, :, :, 0:CK - 1], in_=pX[:, 0:16 * (CK - 1)].rearrange("p (c s j) -> p c s j", c=4, s=SB)),
            [pX_b], [smp_b])
        S.dma('pool', 'd2d', [lambda e: e.dma_start(out=nks[:, 0:R - DS, :], in_=ck[:, DS:R, :]),
                              lambda e: e.dma_start(out=nvs[:, 0:R - DS, :], in_=cv[:, DS:R, :]),
                              lambda e: e.dma_start(out=ncs[:, 0:CK - 1 - DS, :], in_=sc[:, DS:CK - 1, :])])

    uflat = uext.rearrange("p c n -> p (c n)").bitcast(F32)
    sproj = uflat[:, 0:13 * 64]
    sproj_b = Buf("sproj")
    Pb = [it[1] for it in P_items]

    def gacc(m, up):
        return P_items[(3 if up else 0) + m // 8][0][:, (m % 8) * 64:(m % 8) * 64 + 64]

    def mk_gu_hook(f, xin):
        def hook(k, mh, ob, ob_b):
            o3 = ob.rearrange("p (m c) -> p m c", c=256)

            def fn(e):
                ins = None
                for mi in range(11):
                    m = mh * 11 + mi
                    st_ = (k == 0 and m % 8 == 0)
                    e.matmul(gacc(m, False), lhsT=o3[:, mi, 0:128], rhs=xin[0][:, k, 0:64], start=st_, stop=(k == KD - 1), skip_group_check=True)
                    ins = e.matmul(gacc(m, True), lhsT=o3[:, mi, 128:256], rhs=xin[0][:, k, 0:64], start=st_, stop=(k == KD - 1), skip_group_check=True)
                return ins
            pe(fn, [ob_b, xin[1]], Pb)
            if k == KD - 1 and mh == 1:
                for b_ in range(3):
                    nm = 8 if b_ < 2 else NM - 16
                    sg, sg_b = tmpf.next()
                    act(lambda e: e.activation(out=sg[:, 0:nm * 64], in_=P_items[b_][0][:, 0:nm * 64], func=AF.Silu), [Pb[b_]], [sg_b])
                    dve(lambda e: e.tensor_tensor(out=hT[:, 8 * b_:8 * b_ + nm, 0:64], in0=sg[:, 0:nm * 64].rearrange("p (m n) -> p m n", n=64),
                                                  in1=P_items[3 + b_][0][:, 0:nm * 64].rearrange("p (m n) -> p m n", n=64), op=ALU.mult),
                        [sg_b, Pb[3 + b_]], [hT_bA, hT_bB])
        return hook

    def mk_d_hook(f):
        def hook(m0, ob, ob_b):
            def fn(e):
                ins = None
                for mi in range(2):
                    m = m0 + mi
                    for hf in range(2):
                        ins = e.matmul(P_items[hf][0][0:64, :], lhsT=hT[:, m, 0:64], rhs=ob[:, mi * 1024 + hf * 512:mi * 1024 + (hf + 1) * 512],
                                       start=(m == 0), stop=(m == NM - 1))
                return ins
            pe(fn, [ob_b, hT_bA, hT_bB], [Pb[0], Pb[1]])
            if m0 == NM - 2:
                sl = GS['slots'][0]
                for hf in range(2):
                    xsl = xs[sl][0:64, hf * 512:(hf + 1) * 512]
                    dve(lambda e: e.scalar_tensor_tensor(out=xsl, in0=P_items[hf][0][0:64, :], scalar=0.5, in1=xsl, op0=ALU.mult, op1=ALU.add),
                        [Pb[hf], xs_b[sl]], [xs_b[sl]])
                if f == 0:
                    rms_tr(GS, rms_chain(GS, 0), xq[1])
                else:
                    final_norm_store(GS, 0)
        return hook

    pTf = [(pT_ring.items[i][0].bitcast(F32), pT_ring.items[i][1]) for i in range(2)]

    def facc(i):
        return pTf[i // 8][0][:, (i % 8) * 64:(i % 8) * 64 + 64]

    in_cols = [0, 128, 256, 384, 512] + [768 + 256 * c + 128 * t for c in range(4) for t in range(2)]

    def in_hook(k, ob, ob_b):
        xin = xq[1]

        def fn(e):
            ins = None
            for i, c0 in enumerate(in_cols):
                ins = e.matmul(facc(i), lhsT=ob[:, c0:c0 + 128], rhs=xin[0][:, k, 0:64], start=(k == 0 and i % 8 == 0), stop=(k == KD - 1),
                               skip_group_check=True)
            for s_ in range(SB):
                ins = e.matmul(P_items[5][0][0:DS, s_ * 128:(s_ + 1) * 128], lhsT=xin[0][:, k, s_ * DS:(s_ + 1) * DS], rhs=ob[:, 640:768],
                               start=(k == 0 and s_ == 0), stop=(k == KD - 1), skip_group_check=True)
            return ins
        pe(fn, [ob_b, xin[1]], [pTf[0][1], pTf[1][1], Pb[5]])
        if k == KD - 1:
            act(lambda e: e.copy(out=sproj[:, 0:512], in_=pTf[0][0]), [pTf[0][1]], [sproj_b])
            act(lambda e: e.copy(out=sproj[:, 512:832], in_=pTf[1][0][:, 0:320]), [pTf[1][1]], [sproj_b])
            pV = P_items[5][0]
            for s_ in range(SB):
                act(lambda e: e.copy(out=Vsn[:, s_, 0, 0:64], in_=pV[0:DS, s_ * 128:s_ * 128 + 64]), [Pb[5]], [smp_b])
                dve(lambda e: e.tensor_copy(out=Vsn[:, s_, 1, 64:128], in_=pV[0:DS, s_ * 128 + 64:(s_ + 1) * 128]), [Pb[5]], [smp_b])
                vo, vo_b = tmpf.next()
                dve(lambda e: e.tensor_copy(out=vo[0:DS, 0:128], in_=pV[0:DS, s_ * 128:(s_ + 1) * 128]), [Pb[5]], [vo_b])
                S.dma('pool', 'ovo%d' % s_, [lambda e: e.dma_start(out=nvs[s_, R - DS:R, :], in_=vo[0:DS, 0:128])], reads=[vo_b])
            pre = dict(qk=[(sproj[:, i * 64:(i + 1) * 64], sproj_b) for i in range(5)],
                       A=[(sproj[:, (5 + 2 * c) * 64:(6 + 2 * c) * 64], sproj_b) for c in range(4)],
                       G=[(sproj[:, (6 + 2 * c) * 64:(7 + 2 * c) * 64], sproj_b) for c in range(4)])
            mixer(GS, pre)

    def o_hook(j, ob, ob_b):
        def fn(e):
            ins = None
            for i2 in range(2):
                kc = 2 * j + i2
                for hf in range(2):
                    ins = e.matmul(P_items[hf][0][0:64, :], lhsT=catT[:, kc, 0:64], rhs=ob[:, i2 * 1024 + hf * 512:i2 * 1024 + (hf + 1) * 512],
                                   start=(kc == 0), stop=(kc == KD - 1))
            return ins
        pe(fn, [ob_b, catA_b, catC_b], [Pb[0], Pb[1]])
        if j == 3:
            sl = GS['slots'][0]
            for hf in range(2):
                xsl = xs[sl][0:64, hf * 512:(hf + 1) * 512]
                dve(lambda e: e.tensor_tensor(out=xsl, in0=P_items[hf][0][0:64, :], in1=xsl, op=ALU.add), [Pb[hf], xs_b[sl]], [xs_b[sl]])
            rms_tr(GS, rms_chain(GS, 0), xq[0])

    S.barrier()
    x_load(GS, [0])
    sample_prep()
    rms_tr(GS, rms_chain(GS, 0), xq[0])
    pp_gu(0, mk_gu_hook(0, xq[0]))
    pp_d(0, mk_d_hook(0))
    pp_in(in_hook)
    pp_o(o_hook)
    pp_gu(1, mk_gu_hook(1, xq[0]))
    pp_d(1, mk_d_hook(1))
    S.barrier()
    x_load(groups[0], range(groups[0]['NT']))
    for t in range(groups[0]['NT']):
        rms_tr(groups[0], rms_chain(groups[0], t), cur_x())
    for gidx, G in enumerate(groups):
        Gn = groups[gidx + 1] if gidx + 1 < len(groups) else None
        if Gn is not None:
            x_load(Gn, range(Gn['NT']))
        ffn(G, 0, Gn)
        mixer(G)
        ffn(G, 1, Gn)
    S.finish('sp')
    return nc


def _rel_bucket_np(rel):
    rel = np.asarray(rel, dtype=np.int64)
    nb = 16
    max_exact = 8
    ret = np.where(rel > 0, nb, 0)
    n = np.abs(rel)
    nf = np.maximum(n, 1).astype(np.float32)
    large = max_exact + (np.log(nf / np.float32(max_exact)) / np.float32(np.log(128 / max_exact))
                         * np.float32(nb - max_exact)).astype(np.int32)
    large = np.minimum(large, nb - 1)
    return ret + np.where(n < max_exact, n, large)


_NC_CACHE = {}


def kernel(x_prompt, x_sample, cache_k, cache_v, state_conv, rel_bias_table,
           ffn1_norm, ffn1_w_gu, ffn1_w_down, mix_norm, w_in, q_norm, k_norm, sinks,
           conv_w, conv_b, conv_ln_g, conv_ln_b, w_out, ffn2_norm, ffn2_w_gu,
           ffn2_w_down, final_norm):
    f = lambda a: np.ascontiguousarray(np.asarray(a, dtype=np.float32))
    if "nc" not in _NC_CACHE:
        _NC_CACHE["nc"] = build_nc()
    nc = _NC_CACHE["nc"]
    ident = np.eye(128, dtype=np.float32)
    jmat = np.ascontiguousarray(ident[::-1])
    j = np.arange(512)
    bk = _rel_bucket_np(255 - j)
    ohr = np.zeros((32, 512), np.float32)
    ohr[bk, j] = 1.0
    shared = dict(
        rbt=f(rel_bias_table), ffn1_norm=f(ffn1_norm[0:1]), mix_norm=f(mix_norm[0:1]), ffn2_norm=f(ffn2_norm[0:1]),
        ffn1_w_gu=f(ffn1_w_gu[0]), ffn2_w_gu=f(ffn2_w_gu[0]), ffn1_w_down=f(ffn1_w_down[0]), ffn2_w_down=f(ffn2_w_down[0]),
        w_in=f(w_in[0]), q_norm=f(q_norm[0:1]), k_norm=f(k_norm[0:1]), sinks=f(sinks[0:1]), conv_w=f(conv_w[0]),
        conv_b=f(conv_b[0:1]), conv_ln_g=f(conv_ln_g[0:1]), conv_ln_b=f(conv_ln_b[0:1]), w_out=f(w_out[0]),
        final_norm=f(final_norm[0:1]), ident=ident, jmat=jmat, ohr=ohr)
    xpf, xsf = f(x_prompt), f(x_sample)
    ckf, cvf, scf = f(cache_k), f(cache_v), f(state_conv)
    in_maps = []
    for c in range(NCORES):
        m = dict(shared)
        m["xp"] = xpf[NSEQ * c:NSEQ * (c + 1)].reshape(NSEQ * SEQ, D)
        m["xsm"] = xsf[SB * c:SB * (c + 1)].reshape(SB * DS, D)
        m["ck"] = ckf[0, SB * c:SB * (c + 1)].reshape(SB, R, 128)
        m["cv"] = cvf[0, SB * c:SB * (c + 1)].reshape(SB, R, 128)
        m["sc"] = scf[0, SB * c:SB * (c + 1)]
        in_maps.append(m)
    res = run_bass_kernel_spmd(nc, in_maps, core_ids=list(range(NCORES)))
    rs = res.results
    cat = lambda k: np.concatenate([np.asarray(r[k], dtype=np.float32) for r in rs], axis=0)
    y_p = cat("yp").reshape(16, SEQ, D)
    y_s = cat("ys").reshape(32, DS, D)
    nkp = cat("nkp").reshape(1, 16, R, 2, 64)
    nvp = cat("nvp").reshape(1, 16, R, 2, 64)
    ncp = cat("ncp").reshape(1, 16, CK - 1, 512)
    nks = cat("nks").reshape(1, 32, R, 2, 64)
    nvs = cat("nvs").reshape(1, 32, R, 2, 64)
    ncs = cat("ncs").reshape(1, 32, CK - 1, 512)
    return (y_p, y_s, nkp, nvp, ncp, nks, nvs, ncs)
```
